# Optimizing a Trainium2 kernel written in Bass

```python
import jax, jax.numpy as jnp
from jax import lax
import numpy as np

D_MODEL = 1024
BATCH = 16
SEQ = 2048
DEPTH = 4

CHUNK = 64
D_MIX = D_MODEL
SSD_WIDTH = D_MIX // 2
ATTN_WIDTH = D_MIX - SSD_WIDTH
SSD_HEAD_DIM = 64
SSD_HEADS = SSD_WIDTH // SSD_HEAD_DIM
SSD_GROUPS = 2
SSD_HPG = SSD_HEADS // SSD_GROUPS
SSD_STATE = 128
SSD_CONV = 4
SSD_CHUNK = CHUNK
XBC_DIM = SSD_WIDTH + 2 * SSD_GROUPS * SSD_STATE
ATTN_HEAD_DIM = 64
ATTN_HEADS = ATTN_WIDTH // ATTN_HEAD_DIM
LEFT_CHUNKS = 8
BAND = (LEFT_CHUNKS + 1) * CHUNK
MAX_REL = 128
N_REL = 2 * MAX_REL + 1
FFN_DIM = 2816
FFN_CONV = 3
ADA_DIM = 6 * D_MODEL
IN_SIZES = (SSD_WIDTH, XBC_DIM, SSD_HEADS, ATTN_WIDTH, ATTN_WIDTH, ATTN_WIDTH)
IN_DIM = sum(IN_SIZES)
IN_SPLITS = [int(v) for v in np.cumsum(IN_SIZES)[:-1]]
EPS = 1e-6

kernel_name = "hymba_ssd_chunkattn_convffn_adaln"


def rmsnorm(x, w):
    xf = x.astype(jnp.float32)
    y = xf * lax.rsqrt(jnp.mean(xf * xf, axis=-1, keepdims=True) + EPS)
    return (y * w.astype(jnp.float32)).astype(x.dtype)


def causal_dwconv(x, w):
    k, ch = w.shape
    return lax.conv_general_dilated(x, w[:, None, :].astype(x.dtype), window_strides=(1,),
                                    padding=[(k - 1, 0)], dimension_numbers=('NWC', 'WIO', 'NWC'),
                                    feature_group_count=ch)


def ssd_scan(xdt, adt, bm, cm):
    b, s = xdt.shape[:2]
    nc = s // SSD_CHUNK
    L = SSD_CHUNK
    X = xdt.reshape(b, nc, L, SSD_GROUPS, SSD_HPG, SSD_HEAD_DIM)
    A = adt.reshape(b, nc, L, SSD_GROUPS, SSD_HPG)
    Bm = bm.reshape(b, nc, L, SSD_GROUPS, SSD_STATE)
    Cm = cm.reshape(b, nc, L, SSD_GROUPS, SSD_STATE)
    a_cs = jnp.cumsum(A, axis=2)
    seg = a_cs[:, :, :, None] - a_cs[:, :, None]
    causal = jnp.tril(jnp.ones((L, L), dtype=bool))[:, :, None, None]
    decay_ls = jnp.exp(jnp.where(causal, seg, -jnp.inf))
    cb = jnp.einsum('bclgn,bcsgn->bclsg', Cm, Bm)
    y_diag = jnp.einsum('bclsg,bclsge,bcsgep->bclgep', cb, decay_ls, X)
    decay_to_end = jnp.exp(a_cs[:, :, -1:] - a_cs)
    states = jnp.einsum('bclgn,bclge,bclgep->bcgepn', Bm, decay_to_end, X)
    chunk_decay = jnp.exp(a_cs[:, :, -1])

    def step(h, inp):
        st, dec = inp
        return h * dec[..., None, None] + st, h

    _, prev = lax.scan(step, jnp.zeros_like(states[:, 0]),
                       (jnp.moveaxis(states, 1, 0), jnp.moveaxis(chunk_decay, 1, 0)))
    prev = jnp.moveaxis(prev, 0, 1)
    y_off = jnp.einsum('bclgn,bcgepn,bclge->bclgep', Cm, prev, jnp.exp(a_cs))
    return (y_diag + y_off).reshape(b, s, SSD_HEADS, SSD_HEAD_DIM)


def ssd_mixer(z, xbc, dt_raw, conv_w, conv_b, dt_bias, a_log, d_skip, norm_w):
    b, s, _ = xbc.shape
    xbc = jax.nn.silu(causal_dwconv(xbc, conv_w) + conv_b.astype(xbc.dtype))
    xs, bm, cm = jnp.split(xbc.astype(jnp.float32), [SSD_WIDTH, SSD_WIDTH + SSD_GROUPS * SSD_STATE], axis=-1)
    xs = xs.reshape(b, s, SSD_HEADS, SSD_HEAD_DIM)
    bm = bm.reshape(b, s, SSD_GROUPS, SSD_STATE)
    cm = cm.reshape(b, s, SSD_GROUPS, SSD_STATE)
    dt = jax.nn.softplus(dt_raw.astype(jnp.float32) + dt_bias.astype(jnp.float32))
    A = -jnp.exp(a_log.astype(jnp.float32))
    y = ssd_scan(xs * dt[..., None], dt * A, bm, cm)
    y = y + d_skip.astype(jnp.float32)[:, None] * xs
    y = y.reshape(b, s, SSD_WIDTH) * jax.nn.silu(z.astype(jnp.float32))
    return rmsnorm(y, norm_w).astype(z.dtype)


def chunk_attention(q, k, v, rel_bias):
    b, s, _ = q.shape
    nc = s // CHUNK
    pad = LEFT_CHUNKS * CHUNK
    qc = q.reshape(b, nc, CHUNK, ATTN_HEADS, ATTN_HEAD_DIM)

    def band(t):
        tp = jnp.pad(t.reshape(b, s, ATTN_HEADS, ATTN_HEAD_DIM), ((0, 0), (pad, 0), (0, 0), (0, 0)))
        tp = tp.reshape(b, nc + LEFT_CHUNKS, CHUNK, ATTN_HEADS, ATTN_HEAD_DIM)
        return jnp.concatenate([tp[:, w:w + nc] for w in range(LEFT_CHUNKS + 1)], axis=2)

    kb, vb = band(k), band(v)
    scores = jnp.einsum('bcqhd,bckhd->bhcqk', qc, kb).astype(jnp.float32) * (ATTN_HEAD_DIM ** -0.5)
    qi = jnp.arange(CHUNK)[:, None]
    kj = jnp.arange(BAND)[None, :]
    rel_idx = jnp.clip(qi - kj + pad, -MAX_REL, MAX_REL) + MAX_REL
    bias = rel_bias.astype(jnp.float32)[:, rel_idx]
    key_pos = jnp.arange(nc)[:, None] * CHUNK - pad + kj
    valid = (key_pos >= 0)[:, None, :]
    scores = jnp.where(valid, scores + bias[:, None], jnp.finfo(jnp.float32).min)
    p = jax.nn.softmax(scores, axis=-1).astype(v.dtype)
    o = jnp.einsum('bhcqk,bckhd->bcqhd', p, vb)
    return o.reshape(b, s, ATTN_WIDTH)


def conv_ffn(h, w_up, conv_w, w_down):
    u = causal_dwconv(h @ w_up, conv_w)
    g, val = jnp.split(u, 2, axis=-1)
    return (jax.nn.silu(g) * val) @ w_down


def setup_inputs(seed: int = 0) -> dict:
    key = jax.random.key(seed)
    ks = jax.random.split(key, 24)
    f32 = jnp.float32
    nrm = lambda k, shape, sc: jax.random.normal(k, shape, f32) * sc
    u = jax.random.uniform(ks[10], (DEPTH, SSD_HEADS), f32)
    dt0 = jnp.exp(u * (jnp.log(0.1) - jnp.log(0.001)) + jnp.log(0.001))
    return {
        "x": nrm(ks[0], (BATCH, SEQ, D_MODEL), 1.0),
        "c": nrm(ks[1], (BATCH, D_MODEL), 1.0),
        "norm_mix_w": 1.0 + nrm(ks[2], (DEPTH, D_MODEL), 0.05),
        "w_ada": nrm(ks[3], (DEPTH, D_MODEL, ADA_DIM), 0.02),
        "b_ada": nrm(ks[4], (DEPTH, ADA_DIM), 0.02),
        "w_in": nrm(ks[5], (DEPTH, D_MODEL, IN_DIM), D_MODEL ** -0.5),
        "ssd_conv_w": nrm(ks[6], (DEPTH, SSD_CONV, XBC_DIM), SSD_CONV ** -0.5),
        "ssd_conv_b": nrm(ks[7], (DEPTH, XBC_DIM), 0.02),
        "dt_bias": dt0 + jnp.log(-jnp.expm1(-dt0)),
        "a_log": jnp.log(jax.random.uniform(ks[8], (DEPTH, SSD_HEADS), f32, 1.0, 16.0)),
        "d_skip": 1.0 + nrm(ks[9], (DEPTH, SSD_HEADS), 0.1),
        "ssd_norm_w": 1.0 + nrm(ks[11], (DEPTH, SSD_WIDTH), 0.05),
        "rel_bias": nrm(ks[12], (DEPTH, ATTN_HEADS, N_REL), 0.5),
        "w_out": nrm(ks[13], (DEPTH, D_MIX, D_MODEL), D_MIX ** -0.5),
        "norm_ffn_w": 1.0 + nrm(ks[14], (DEPTH, D_MODEL), 0.05),
        "w_up": nrm(ks[15], (DEPTH, D_MODEL, 2 * FFN_DIM), D_MODEL ** -0.5),
        "ffn_conv_w": nrm(ks[16], (DEPTH, FFN_CONV, 2 * FFN_DIM), FFN_CONV ** -0.5),
        "w_down": nrm(ks[17], (DEPTH, FFN_DIM, D_MODEL), FFN_DIM ** -0.5),
        "final_norm_w": 1.0 + nrm(ks[18], (D_MODEL,), 0.05),
    }


def reference(x, c, norm_mix_w, w_ada, b_ada, w_in, ssd_conv_w, ssd_conv_b, dt_bias, a_log, d_skip,
              ssd_norm_w, rel_bias, w_out, norm_ffn_w, w_up, ffn_conv_w, w_down, final_norm_w):
    c_act = jax.nn.silu(c)
    for l in range(DEPTH):
        mod = (c_act @ w_ada[l] + b_ada[l])[:, None, :]
        sh_m, sc_m, g_m, sh_f, sc_f, g_f = jnp.split(mod, 6, axis=-1)
        h = rmsnorm(x, norm_mix_w[l]) * (1 + sc_m) + sh_m
        z, xbc, dt_raw, q, k, v = jnp.split(h @ w_in[l], IN_SPLITS, axis=-1)
        y_ssd = ssd_mixer(z, xbc, dt_raw, ssd_conv_w[l], ssd_conv_b[l], dt_bias[l], a_log[l],
                          d_skip[l], ssd_norm_w[l])
        y_att = chunk_attention(q, k, v, rel_bias[l])
        x = x + g_m * (jnp.concatenate([y_ssd, y_att], axis=-1) @ w_out[l])
        h = rmsnorm(x, norm_ffn_w[l]) * (1 + sc_f) + sh_f
        x = x + g_f * conv_ffn(h, w_up[l], ffn_conv_w[l], w_down[l])
    return rmsnorm(x, final_norm_w)
```

```python
import numpy as np
import concourse.bass as bass
import concourse.mybir as mybir
from concourse.bass_utils import run_bass_kernel_spmd
from contextlib import ExitStack

F32 = mybir.dt.float32
BF16 = mybir.dt.bfloat16
AF = mybir.ActivationFunctionType
ALU = mybir.AluOpType
AX = mybir.AxisListType


class Sched:
    def __init__(self, nc, es, ring=8):
        self.nc = nc
        self.eobj = {'pe': nc.tensor, 'act': nc.scalar, 'dve': nc.vector, 'pool': nc.gpsimd, 'sp': nc.sync}
        self.prog = {k: [] for k in self.eobj}
        self.sem = {k: es.enter_context(nc.semaphore("s_" + k)) for k in ('pe', 'act', 'dve', 'pool')}
        self.cnt = {k: 0 for k in self.sem}
        self.ring = {q: [es.enter_context(nc.semaphore("d_%s%d" % (q, i))) for i in range(ring)] for q in ('sp', 'pool')}
        self.ringval = {q: [0] * ring for q in self.ring}
        self.ringidx = {q: 0 for q in self.ring}
        self.known = {k: {} for k in self.eobj}
        self.snap = {}
        self.last_w = {}
        self.readers = {}
        self.nwaits = 0
        self.nops = 0

    def _deps(self, reads, writes):
        toks = {}
        for k in reads:
            w = self.last_w.get(k)
            if w is not None:
                toks[(w[0], w[2])] = w
        for k in writes:
            w = self.last_w.get(k)
            if w is not None:
                toks[(w[0], w[2])] = w
            for r in self.readers.get(k, ()):
                toks[(r[0], r[2])] = r
        return sorted(toks.values(), key=lambda t: -t[2])

    def _wait(self, eng, tok):
        name, sem, val = tok
        kn = self.known[eng]
        if kn.get(name, 0) >= val:
            return
        if eng == 'pe' and name == 'pe':
            return
        self.prog[eng].append(lambda E, sem=sem, val=val: E.wait_ge(sem, val))
        self.nwaits += 1
        kn[name] = val
        sn = self.snap.get((name, val))
        if sn:
            for n2, v2 in sn.items():
                if kn.get(n2, 0) < v2:
                    kn[n2] = v2

    def _record(self, tok, reads, writes):
        for k in reads:
            self.readers.setdefault(k, []).append(tok)
        for k in writes:
            self.last_w[k] = tok
            self.readers[k] = []

    def op(self, eng, fn, reads=(), writes=()):
        for tok in self._deps(reads, writes):
            self._wait(eng, tok)
        self.cnt[eng] += 1
        v = self.cnt[eng]
        sem = self.sem[eng]
        self.prog[eng].append(lambda E, fn=fn, sem=sem: fn(E).then_inc(sem, 1))
        tok = (eng, sem, v)
        self.snap[(eng, v)] = dict(self.known[eng])
        self._record(tok, reads, writes)
        self.nops += 1
        return tok

    def dma(self, q, out, in_, reads=(), writes=()):
        for tok in self._deps(reads, writes):
            self._wait(q, tok)
        j = self.ringidx[q]
        self.ringidx[q] = (j + 1) % len(self.ring[q])
        sem = self.ring[q][j]
        name = "%s_ring%d" % (q, j)
        prev = self.ringval[q][j]
        if prev:
            self._wait(q, (name, sem, prev))
        val = prev + 16
        self.ringval[q][j] = val
        self.prog[q].append(lambda E, out=out, in_=in_, sem=sem: E.dma_start(out=out, in_=in_).then_inc(sem, 16))
        tok = (name, sem, val)
        self._record(tok, reads, writes)
        return tok

    def finish(self, keys, eng='sp'):
        for k in keys:
            w = self.last_w.get(k)
            if w is not None:
                self._wait(eng, w)
        nc = self.nc
        with nc.Block() as block:
            for name, deco in (('sp', block.sync), ('pe', block.tensor), ('act', block.scalar),
                               ('dve', block.vector), ('pool', block.gpsimd)):
                lst = self.prog[name]

                def body(E, lst=lst):
                    for f in lst:
                        f(E)
                deco(body)

    def barrier(self):
        toks = [(k, self.sem[k], self.cnt[k]) for k in self.sem if self.cnt[k] > 0]
        for q in self.ring:
            for j, sem in enumerate(self.ring[q]):
                if self.ringval[q][j]:
                    toks.append(("%s_ring%d" % (q, j), sem, self.ringval[q][j]))
        for eng in self.eobj:
            for tok in toks:
                self._wait(eng, tok)
        self.last_w = {}
        self.readers = {}

    def tt(self, eng, out, in0, in1, op, r, w):
        return self.op(eng, lambda e: e.tensor_tensor(out=out, in0=in0, in1=in1, op=op), r, w)

    def ts(self, eng, out, in0, s1, s2, op0, op1, r, w):
        if s2 is None:
            return self.op(eng, lambda e: e.tensor_scalar(out=out, in0=in0, scalar1=s1, scalar2=None, op0=op0), r, w)
        return self.op(eng, lambda e: e.tensor_scalar(out=out, in0=in0, scalar1=s1, scalar2=s2, op0=op0, op1=op1), r, w)

    def stt(self, eng, out, in0, scalar, in1, op0, op1, r, w):
        return self.op(eng, lambda e: e.scalar_tensor_tensor(out=out, in0=in0, scalar=scalar, in1=in1, op0=op0, op1=op1), r, w)

    def act(self, out, in_, func, r, w, bias=None, scale=None):
        kw = {}
        if bias is not None:
            kw['bias'] = bias
        if scale is not None:
            kw['scale'] = scale
        return self.op('act', lambda e: e.activation(out=out, in_=in_, func=func, **kw), r, w)

    def cp(self, eng, out, in_, r, w):
        if eng == 'act':
            return self.op(eng, lambda e: e.activation(out=out, in_=in_, func=AF.Copy), r, w)
        return self.op(eng, lambda e: e.tensor_copy(out=out, in_=in_), r, w)

    def mm(self, lst, r, w):
        def f(e):
            for (o, l, rh, st, sp) in lst:
                i = e.matmul(o, lhsT=l, rhs=rh, start=st, stop=sp)
            return i
        return self.op('pe', f, r, w)

    def tr(self, lst, r, w):
        def f(e):
            for (o, i_, idn) in lst:
                i = e.transpose(o, i_, idn)
            return i
        return self.op('pe', f, r, w)

    def memset(self, eng, ap, val, w):
        return self.op(eng, lambda e: e.memset(ap, val), (), w)


class Arena:
    def __init__(self, nc, es, nbytes):
        self.t = es.enter_context(nc.sbuf_tensor("arena", [128, nbytes // 4], F32))
        self.nbytes = nbytes
        self.off = 0

    def alloc(self, n, dt):
        sz = n * (4 if dt == F32 else 2)
        sz = (sz + 31) // 32 * 32
        assert self.off + sz <= self.nbytes, ("arena overflow", self.off, sz, self.nbytes)
        ap = self.t[:, self.off // 4:(self.off + sz) // 4]
        if dt != F32:
            ap = ap.bitcast(dt)
        self.off += sz
        return ap[:, 0:n]


D = 1024
SL = 2048
NL = 4
TT = 512
NTILE = SL // TT
IN_DIM = 3080
FFN = 2816
NJ = FFN // 128
EPS = 1e-6
NEG = -30000.0


def build_program(nlayers=NL, nseq=2, flags=('ssd', 'att', 'ffn')):
    nc = bass.Bass("TRN2", target_bir_lowering=False)

    def din(name, shape):
        return nc.dram_tensor(name, shape, F32, kind="ExternalInput").ap()

    x_t = din("x_t", [nseq, D, SL])
    c_t = din("c_t", [128, 8 * nseq])
    w_ada = din("w_ada", [NL, D, 6 * D])
    b_ada_t = din("b_ada_t", [128, NL * 48])
    w_in = din("w_in", [NL, D, IN_DIM])
    w_out = din("w_out", [NL, D, D])
    w_up = din("w_up", [NL, D, 2 * FFN])
    w_down = din("w_down", [NL, FFN, D])
    nmw_t = din("nmw_t", [128, NL * 8])
    nfw_t = din("nfw_t", [128, NL * 8])
    fnw_t = din("fnw_t", [128, 8])
    convw_t = din("convw_t", [128, NL * 32])
    convb_t = din("convb_t", [128, NL * 8])
    fconvw_t = din("fconvw_t", [128, NL * 132])
    dtb_bc = din("dtb_bc", [128, NL * 8])
    alog_bc = din("alog_bc", [128, NL * 8])
    dskx_d = din("dskx", [NL, 128, 512])
    ssdnw_d = din("ssdnw", [NL, 128, 512])
    biasT_d = din("biasT", [NL, 128, 8 * 640])
    identf_d = din("identf", [128, 128])
    u1_d = din("u1", [128, 128])
    u2_d = din("u2", [128, 128])
    mask01_d = din("mask01", [128, 128])
    out_t = nc.dram_tensor("out_t", [nseq, D, SL], F32, kind="ExternalOutput").ap()
    win_s = nc.dram_tensor("win_s", [NL, D, IN_DIM], BF16, kind="Internal").ap()
    wout_s = nc.dram_tensor("wout_s", [NL, D, D], BF16, kind="Internal").ap()
    wup_s = nc.dram_tensor("wup_s", [NL, D, 2 * FFN], BF16, kind="Internal").ap()
    wdn_s = nc.dram_tensor("wdn_s", [NL, FFN, D], BF16, kind="Internal").ap()

    with ExitStack() as es:
        S = Sched(nc, es)
        AR = Arena(nc, es, 207360)
        ps = es.enter_context(nc.psum_tensor("ps", [128, 4096], F32))
        dbg_done = set()

        def dbgdump(name, ap, keys):
            if 'dbg' not in flags or name in dbg_done:
                return
            dbg_done.add(name)
            d = nc.dram_tensor("dbg_" + name, list(ap.shape), ap.dtype, kind="ExternalOutput").ap()
            S.dma('sp', d, ap, reads=keys, writes=[('dbg', name)])

        def bank(i):
            return ps[:, i * 512:(i + 1) * 512]
        bk = [0]
        pk = [0]

        singles = [[6, 7]]

        def nb():
            lst = singles[0]
            i = bk[0] % len(lst)
            bk[0] = i + 1
            return lst[i]

        def npair():
            i = pk[0]
            pk[0] = (i + 1) % 2
            return i

        xT = AR.alloc(8 * SL, F32)
        xT3 = xT.rearrange("p (c n) -> p c n", c=8)
        identf = AR.alloc(128, F32)
        identb = AR.alloc(128, BF16)
        onesb = AR.alloc(128, BF16)
        onesf = AR.alloc(128, F32)
        u1 = AR.alloc(128, F32)
        u2 = AR.alloc(128, F32)
        mask01 = AR.alloc(128, F32)
        mod = AR.alloc(NL * 48 * nseq, F32)
        mod4 = mod.rearrange("p (l j b) -> p l j b", l=NL, j=48)
        Am = AR.alloc(NL * 8 * nseq, F32)
        Am4 = Am.rearrange("p (l k b) -> p l k b", l=NL, k=8)
        Af = AR.alloc(NL * 8 * nseq, F32)
        Af4 = Af.rearrange("p (l k b) -> p l k b", l=NL, k=8)
        cact = AR.alloc(8 * nseq, F32)
        cact3 = cact.rearrange("p (k b) -> p k b", k=8)
        bada = AR.alloc(NL * 48, F32)
        bada3 = bada.rearrange("p (l j) -> p l j", l=NL)
        nmw = AR.alloc(NL * 8, F32)
        nfw = AR.alloc(NL * 8, F32)
        fnw = AR.alloc(8, F32)
        convw = AR.alloc(NL * 32, F32)
        convb = AR.alloc(NL * 8, F32)
        fconvw = AR.alloc(NL * 132, F32)
        dtb = AR.alloc(NL * 8, F32)
        Aneg = AR.alloc(NL * 8, F32)
        hT = AR.alloc(8 * TT, BF16)
        hT3 = hT.rearrange("p (c n) -> p c n", c=8)
        sq = AR.alloc(2 * TT, BF16)
        hn = AR.alloc(2 * TT, F32)
        rstd = AR.alloc(TT, F32)
        persist_end = AR.off

        for dst, src, k in ((identf, identf_d, 'identf'), (u1, u1_d, 'u1'), (u2, u2_d, 'u2'), (mask01, mask01_d, 'mask01'),
                            (bada, b_ada_t, 'bada'), (nmw, nmw_t, 'nmw'), (nfw, nfw_t, 'nfw'), (fnw, fnw_t, 'fnw'),
                            (convw, convw_t, 'convw'), (convb, convb_t, 'convb'), (fconvw, fconvw_t, 'fconvw'),
                            (dtb, dtb_bc, 'dtb'), (Aneg, alog_bc, 'Aneg'), (cact, c_t, 'cact')):
            S.dma('sp', dst, src, writes=[k])
        S.cp('dve', identb, identf, ['identf'], ['identb'])
        S.memset('dve', onesb, 1.0, ['onesb'])
        S.memset('dve', onesf, 1.0, ['onesf'])
        S.act(Aneg, Aneg, AF.Exp, ['Aneg'], ['Aneg'])
        S.ts('dve', Aneg, Aneg, -1.0, None, ALU.mult, None, ['Aneg'], ['Aneg'])
        S.act(cact, cact, AF.Silu, ['cact'], ['cact'])

        def conv_items(l):
            items = []
            for kc in range(8):
                items.append((win_s[l, kc * 128:(kc + 1) * 128, :], w_in[l, kc * 128:(kc + 1) * 128, :], ('win', l, kc)))
            for kc in range(8):
                items.append((wout_s[l, kc * 128:(kc + 1) * 128, :], w_out[l, kc * 128:(kc + 1) * 128, :], ('wout', l, kc)))
            for kc in range(8):
                items.append((wup_s[l, kc * 128:(kc + 1) * 128, :], w_up[l, kc * 128:(kc + 1) * 128, :], ('wup', l, kc)))
            for jc in range(NJ):
                items.append((wdn_s[l, jc * 128:(jc + 1) * 128, :], w_down[l, jc * 128:(jc + 1) * 128, :], ('wdn', l, jc)))
            return items
        pending = []
        converted = set()
        hoisted = {}

        def pump(n):
            for _ in range(n):
                if not pending:
                    return
                o, i_, k = pending.pop(0)
                S.dma('pool', o, i_, writes=[k])
                converted.add(k)
        for kc in range(8):
            S.dma('sp', xT3[:, kc, :], x_t[0, kc * 128:(kc + 1) * 128, :], writes=[('x', kc, t) for t in range(NTILE)])
        for l_ in range(nlayers):
            pending.extend(conv_items(l_))
        pump(16)

        cactb = AR.alloc(8 * nseq, BF16)
        cactb3 = cactb.rearrange("p (k b) -> p k b", k=8)
        S.cp('dve', cactb, cact, ['cact'], ['cactb'])
        persist_end = AR.off

        def ada_chunk(l, ch, slots, fp32=False):
            wsrc = w_ada[l].rearrange("(kc p) n -> p kc n", p=128)
            slot = ch % 2
            w3 = slots[slot].rearrange("p (k n) -> p k n", k=8)
            S.dma('sp' if fp32 else 'pool', w3, wsrc[:, :, ch * 256:(ch + 1) * 256], writes=[('wada', slot)])
            b_ = nb()
            lst = []
            rhs3 = cact3 if fp32 else cactb3
            for jj in range(2):
                for kc in range(8):
                    lst.append((bank(b_)[:, jj * nseq:(jj + 1) * nseq], w3[:, kc, jj * 128:(jj + 1) * 128], rhs3[:, kc, :], kc == 0, kc == 7))
            S.mm(lst, [('wada', slot), 'cactb', 'cact'], [('ps', b_)])
            S.tt('dve', mod4[:, l, ch * 2:(ch + 1) * 2, :], bank(b_)[:, 0:2 * nseq].rearrange("p (j b) -> p j b", j=2),
                 bada3[:, l, ch * 2:(ch + 1) * 2].unsqueeze(2).to_broadcast([128, 2, nseq]), ALU.add, [('ps', b_), 'bada'], [('mod', l)])

        def ada_finish(l):
            nm3 = nmw.rearrange("p (l k) -> p l k", l=NL)
            nf3 = nfw.rearrange("p (l k) -> p l k", l=NL)
            S.stt('dve', Am4[:, l], mod4[:, l, 8:16, :], 1.0, nm3[:, l, :].unsqueeze(2).to_broadcast([128, 8, nseq]), ALU.add, ALU.mult,
                  [('mod', l), 'nmw'], [('Am', l)])
            S.stt('dve', Af4[:, l], mod4[:, l, 32:40, :], 1.0, nf3[:, l, :].unsqueeze(2).to_broadcast([128, 8, nseq]), ALU.add, ALU.mult,
                  [('mod', l), 'nfw'], [('Af', l)])

        ada_mark = AR.off
        slots0 = [AR.alloc(8 * 256, F32), AR.alloc(8 * 256, F32)]
        for ch in range(24):
            ada_chunk(0, ch, slots0, fp32=True)
        ada_finish(0)
        S.barrier()
        AR.off = ada_mark
        phase_base = AR.off

        def norm_tile(t, Aap, shap, extra=()):
            extra = list(extra)
            c0 = t * TT
            b = nb()
            for kc in range(8):
                sl = sq[:, (kc % 2) * TT:(kc % 2 + 1) * TT]
                S.act(sl, xT3[:, kc, c0:c0 + TT], AF.Square, [('x', kc, t)], [('sq', kc % 2)])
                S.mm([(bank(b), onesb, sl, kc == 0, kc == 7)], [('sq', kc % 2)], [('ps', b)])
            S.act(rstd, bank(b), AF.Ln, [('ps', b)], ['rstd'], bias=EPS, scale=1.0 / D)
            S.act(rstd, rstd, AF.Exp, ['rstd'], ['rstd'], scale=-0.5)
            for kc in range(8):
                hs = hn[:, (kc % 2) * TT:(kc % 2 + 1) * TT]
                S.stt('dve', hs, xT3[:, kc, c0:c0 + TT], Aap(kc), rstd, ALU.mult, ALU.mult, [('x', kc, t), 'rstd'] + extra, [('hn', kc % 2)])
                S.act(hT3[:, kc, :], hs, AF.Identity, [('hn', kc % 2)] + extra, ['hT'], bias=shap(kc))

        def mixer_phase(b, l):
            AR.off = phase_base
            singles[0] = [5, 6, 7]
            kT = AR.alloc(4 * 1024, BF16)
            kT3 = kT.rearrange("p (c n) -> p c n", c=4)
            v1 = AR.alloc(8 * 8 * 65, BF16)
            v1v = v1.rearrange("p (s h e) -> p s h e", s=8, h=8)
            biasT = AR.alloc(8 * 640, BF16)
            biasT3 = biasT.rearrange("p (h n) -> p h n", h=8)
            wdtb = AR.alloc(64, BF16)
            wdtb3 = wdtb.rearrange("p (k n) -> p k n", k=8)
            wring = [AR.alloc(8 * 256, BF16) for _ in range(4)]
            qT = AR.alloc(4 * TT, BF16)
            qT3 = qT.rearrange("p (c n) -> p c n", c=4)
            xraw = AR.alloc(2 * 516, F32)
            halo = AR.alloc(8 * 3, F32)
            halo3 = halo.rearrange("p (c k) -> p c k", c=8)
            cacc = AR.alloc(2 * TT, F32)
            xbcT = AR.alloc(8 * TT, BF16)
            xbcT3 = xbcT.rearrange("p (c n) -> p c n", c=8)
            xs_tm = AR.alloc(2 * 512, BF16)
            B_tm = AR.alloc(2 * 256, BF16)
            sz = AR.alloc(4 * 512, BF16)
            dtt = AR.alloc(4 * 8, F32)
            adt = AR.alloc(4 * 8, F32)
            dtmp = AR.alloc(32, F32)
            rhsU2x = [AR.alloc(1024, F32), AR.alloc(1024, F32)]
            DT = AR.alloc(1024, F32)
            CBm = AR.alloc(256, F32)
            MT = AR.alloc(1024, BF16)
            Xb = AR.alloc(512, BF16)
            Xwb = AR.alloc(512, BF16)
            ecsx = AR.alloc(48, F32)
            wdt = AR.alloc(8, F32)
            t1 = AR.alloc(512, F32)
            t2 = AR.alloc(512, F32)
            prevT = AR.alloc(512, F32)
            prevTb = AR.alloc(512, BF16)
            ssq = AR.alloc(2, F32)
            PT = AR.alloc(2 * 640, BF16)
            rec = AR.alloc(8, F32)
            ymix = AR.alloc(2 * 1024, BF16)
            yT = AR.alloc(8 * TT, BF16)
            yT3 = yT.rearrange("p (c n) -> p c n", c=8)
            dskx = AR.alloc(512, F32)
            ssdnw = AR.alloc(512, F32)

            S.dma('pool', biasT, biasT_d[l], writes=['biasT'])
            S.dma('sp', dskx, dskx_d[l], writes=['dskx'])
            S.dma('sp', ssdnw, ssdnw_d[l], writes=['ssdnw'])
            win_v = win_s[l].rearrange("(kc p) n -> p kc n", p=128)
            wout_v = wout_s[l].rearrange("(kc p) n -> p kc n", p=128)
            win_keys = [('win', l, kc) for kc in range(8)]
            wout_keys = [('wout', l, kc) for kc in range(8)]
            while not all(k_ in converted for k_ in win_keys + wout_keys):
                pump(1)
            S.dma('sp', wdtb3, win_v[:, :, 1536:1544], reads=win_keys, writes=['wdtb'])
            S.memset('pool', v1v[:, :, :, 64:65], 1.0, ['v1'])
            S.memset('pool', prevT, 0.0, ['prevT'])
            S.memset('pool', prevTb, 0.0, ['prevTb'])
            S.memset('pool', halo, 0.0, ['halo'])
            convw4 = convw.rearrange("p (l c k) -> p l c k", l=NL, c=8)
            convb3 = convb.rearrange("p (l c) -> p l c", l=NL)
            dtb3 = dtb.rearrange("p (l h) -> p l h", l=NL)
            Aneg3 = Aneg.rearrange("p (l h) -> p l h", l=NL)
            rs = [0]

            def load_ring(src_v, c0, rkeys):
                slot = rs[0] % 4
                rs[0] += 1
                w3 = wring[slot].rearrange("p (k n) -> p k n", k=8)
                S.dma('sp', w3, src_v[:, :, c0:c0 + 256], reads=rkeys, writes=[('wring', slot)])
                return slot, w3

            free_banks = [5, 6, 7]

            def alloc1():
                assert free_banks, "out of PSUM banks"
                return free_banks.pop(0)

            def free1(bi):
                free_banks.append(bi)

            def proj_fm(w3, slot, oc):
                bq = alloc1()
                S.mm([(bank(bq), w3[:, kc, oc * 128:(oc + 1) * 128], hT3[:, kc, :], kc == 0, kc == 7) for kc in range(8)],
                     ['hT', ('wring', slot)], [('ps', bq)])
                return bq

            def proj_tm(w3, slot, blk):
                bq = alloc1()
                S.mm([(bank(bq)[:, 0:256], hT3[:, kc, blk * 128:(blk + 1) * 128], w3[:, kc, :], kc == 0, kc == 7) for kc in range(8)],
                     ['hT', ('wring', slot)], [('ps', bq)])
                return bq

            def projections(t):
                bd = alloc1()
                S.mm([(bank(bd)[:, blk * 8:(blk + 1) * 8], hT3[:, kc, blk * 128:(blk + 1) * 128], wdtb3[:, kc, :], kc == 0, kc == 7)
                      for blk in range(4) for kc in range(8)], ['hT', 'wdtb'], [('ps', bd)])
                dkeys = [('dt', blk) for blk in range(4)]
                akeys = [('adt', blk) for blk in range(4)]
                S.tt('dve', dtmp.rearrange("p (j h) -> p j h", j=4), bank(bd)[:, 0:32].rearrange("p (j h) -> p j h", j=4),
                     dtb3[:, l, :].unsqueeze(1).to_broadcast([128, 4, 8]), ALU.add, [('ps', bd)], ['dtmp'])
                free1(bd)
                S.act(dtmp, dtmp, AF.Exp, ['dtmp'], ['dtmp'])
                S.act(dtt, dtmp, AF.Ln, ['dtmp'], dkeys, bias=1.0)
                S.tt('dve', adt.rearrange("p (j h) -> p j h", j=4), dtt.rearrange("p (j h) -> p j h", j=4),
                     Aneg3[:, l, :].unsqueeze(1).to_broadcast([128, 4, 8]), ALU.mult, dkeys, akeys)
                zs = [load_ring(win_v, 0, win_keys), load_ring(win_v, 256, win_keys)]
                for blk in range(4):
                    for hf in range(2):
                        slot, w3 = zs[hf]
                        bz = proj_tm(w3, slot, blk)
                        S.act(sz[:, blk * 512 + hf * 256:blk * 512 + (hf + 1) * 256], bank(bz)[:, 0:256], AF.Silu, [('ps', bz)], [('sz', blk)])
                        free1(bz)
                kbase = ((4 * t) % 8) * 128

                def conv_chunk(ci):
                    slot, w3 = load_ring(win_v, 512 + ci * 256, win_keys)
                    for oc in range(2):
                        c = ci * 2 + oc
                        bx = proj_fm(w3, slot, oc)
                        par = c % 2
                        xr = xraw[:, par * 516:par * 516 + 515]
                        ca = cacc[:, par * TT:(par + 1) * TT]
                        S.cp('pool', xr[:, 0:3], halo3[:, c, :], ['halo'], [('xraw', par)])
                        S.act(xr[:, 3:515], bank(bx), AF.Copy, [('ps', bx)], [('xraw', par)])
                        S.act(ca, bank(bx), AF.Copy, [('ps', bx)], [('cacc', par)], scale=convw4[:, l, c, 3:4])
                        free1(bx)
                        S.cp('pool', halo3[:, c, :], xr[:, 512:515], [('xraw', par)], ['halo'])
                        for k in range(3):
                            S.stt('dve', ca, xr[:, k:k + 512], convw4[:, l, c, k:k + 1], ca, ALU.mult, ALU.add, [('xraw', par), ('cacc', par)], [('cacc', par)])
                        S.act(xbcT3[:, c, :], ca, AF.Silu, [('cacc', par)], [('xbcT', c)], bias=convb3[:, l, c:c + 1])

                def q_chunk(ci):
                    slot, w3 = load_ring(win_v, 1544 + ci * 256, win_keys)
                    for oc2 in range(2):
                        oc = ci * 2 + oc2
                        bq = proj_fm(w3, slot, oc2)
                        S.act(qT3[:, oc, :], bank(bq), AF.Copy, [('ps', bq)], ['qT'], scale=0.125)
                        free1(bq)

                def k_chunk(ci):
                    slot, w3 = load_ring(win_v, 2056 + ci * 256, win_keys)
                    for oc2 in range(2):
                        oc = ci * 2 + oc2
                        bq = proj_fm(w3, slot, oc2)
                        S.cp('dve', kT3[:, oc, kbase:kbase + 512], bank(bq), [('ps', bq)], [('kT', (4 * t + j) % 8) for j in range(4)])
                        free1(bq)

                conv_chunk(0)
                q_chunk(0)
                conv_chunk(1)
                q_chunk(1)
                conv_chunk(2)
                k_chunk(0)
                conv_chunk(3)
                k_chunk(1)
                vs = [load_ring(win_v, 2568, win_keys), load_ring(win_v, 2824, win_keys)]
                for blk in range(4):
                    ks = (4 * t + blk) % 8
                    for hf in range(2):
                        slot, w3 = vs[hf]
                        bv = proj_tm(w3, slot, blk)
                        S.cp('dve', v1v[:, ks, hf * 4:(hf + 1) * 4, 0:64], bank(bv)[:, 0:256].rearrange("p (h e) -> p h e", h=4), [('ps', bv), 'v1'], [('v1', ks)])
                        free1(bv)
                emit_rhsU2(0)
                ssd_pre(0)

            zv_w = {}

            def zv_gen(t, blk):
                for hf in range(2):
                    slot, w3 = zv_w['z'][hf]
                    bz = proj_tm(w3, slot, blk)
                    S.act(sz[:, blk * 512 + hf * 256:blk * 512 + (hf + 1) * 256], bank(bz)[:, 0:256], AF.Silu, [('ps', bz)], [('sz', blk)])
                    free1(bz)
                yield
                ks = (4 * t + blk) % 8
                for hf in range(2):
                    slot, w3 = zv_w['v'][hf]
                    bv = proj_tm(w3, slot, blk)
                    S.cp('dve', v1v[:, ks, hf * 4:(hf + 1) * 4, 0:64], bank(bv)[:, 0:256].rearrange("p (h e) -> p h e", h=4), [('ps', bv), 'v1'], [('v1', ks)])
                    free1(bv)
                yield

            def emit_rhsU2(blk):
                r_ = rhsU2x[blk % 2]
                S.tt('pool', r_.rearrange("p (h n) -> p h n", h=8), u2.unsqueeze(1).to_broadcast([128, 8, 128]),
                     adt[:, blk * 8:(blk + 1) * 8].unsqueeze(2).to_broadcast([128, 8, 128]), ALU.mult, [('adt', blk)], [('rhsU2', blk % 2)])

            def ssd_pre(blk):
                bc0 = blk * 128
                par = blk % 2
                rhsU2 = rhsU2x[par]
                ecs = ecsx[:, par * 24:(par + 1) * 24]
                btr = alloc1()
                pb16 = bank(btr).bitcast(BF16)
                S.tr([(pb16[:, c * 128:(c + 1) * 128], xbcT3[:, c, bc0:bc0 + 128], identb) for c in range(6)],
                     [('xbcT', c) for c in range(6)], [('ps', btr)])
                S.cp('act', xs_tm[:, par * 512:(par + 1) * 512], pb16[:, 0:512], [('ps', btr)], [('xs_tm', par)])
                S.cp('act', B_tm[:, par * 256:(par + 1) * 256], pb16[:, 512:768], [('ps', btr)], [('B_tm', par)])
                free1(btr)
                adtb = adt[:, blk * 8:(blk + 1) * 8]
                bs = alloc1()
                S.mm([(bank(bs)[:, 0:8], u2, adtb, True, True), (bank(bs)[:, 8:16], u1, adtb, True, True),
                      (bank(bs)[:, 16:24], onesf, adtb, True, True)], [('adt', blk)], [('ps', bs)])
                S.act(ecs, bank(bs)[:, 0:24], AF.Exp, [('ps', bs)], [('ecs', par)])
                free1(bs)
                sa, sb_ = alloc1(), alloc1()
                S.mm([(bank(sa), u1, rhsU2[:, 0:512], True, True), (bank(sb_), u1, rhsU2[:, 512:1024], True, True)],
                     [('rhsU2', par)], [('ps', sa), ('ps', sb_)])
                S.act(DT[:, 0:512], bank(sa), AF.Exp, [('ps', sa)], ['DT'])
                S.act(DT[:, 512:1024], bank(sb_), AF.Exp, [('ps', sb_)], ['DT'])
                free1(sa)
                free1(sb_)
                bcb = alloc1()
                S.mm([(bank(bcb)[:, g * 128:(g + 1) * 128], xbcT3[:, 4 + g, bc0:bc0 + 128], xbcT3[:, 6 + g, bc0:bc0 + 128], True, True) for g in range(2)],
                     [('xbcT', c) for c in range(4, 8)], [('ps', bcb)])
                S.tt('dve', CBm.rearrange("p (g n) -> p g n", g=2), bank(bcb)[:, 0:256].rearrange("p (g n) -> p g n", g=2),
                     mask01.unsqueeze(1).to_broadcast([128, 2, 128]), ALU.mult, [('ps', bcb)], ['CBm'])
                free1(bcb)

            def ssd_gen(t, blk):
                bc0 = blk * 128
                par = blk % 2
                ym = ymix[:, par * 1024:(par + 1) * 1024]
                ecs = ecsx[:, par * 24:(par + 1) * 24]
                xsb = xs_tm[:, par * 512:(par + 1) * 512]
                xsb3 = xsb.rearrange("p (h e) -> p h e", h=8)
                dtb_ = dtt[:, blk * 8:(blk + 1) * 8]
                S.tt('dve', MT.rearrange("p (g e n) -> p g e n", g=2, e=4), DT.rearrange("p (g e n) -> p g e n", g=2, e=4),
                     CBm.rearrange("p (g n) -> p g n", g=2).unsqueeze(2).to_broadcast([128, 2, 4, 128]), ALU.mult, ['DT', 'CBm'], ['MT'])
                S.tt('dve', wdt, ecs[:, 8:16], dtb_, ALU.mult, [('ecs', par), ('dt', blk)], ['wdt'])
                S.tt('pool', Xb.rearrange("p (h e) -> p h e", h=8), xsb3, dtb_.unsqueeze(2).to_broadcast([128, 8, 64]), ALU.mult,
                     [('xs_tm', par), ('dt', blk)], ['Xb'])
                S.tt('pool', Xwb.rearrange("p (h e) -> p h e", h=8), xsb3, wdt.unsqueeze(2).to_broadcast([128, 8, 64]), ALU.mult,
                     [('xs_tm', par), 'wdt'], ['Xwb'])
                S.tt('pool', t2, xsb, dskx, ALU.mult, [('xs_tm', par), 'dskx'], ['t2'])
                if blk < 3:
                    emit_rhsU2(blk + 1)
                yield
                by = alloc1()
                S.mm([(bank(by)[:, h * 64:(h + 1) * 64], MT[:, h * 128:(h + 1) * 128], Xb[:, h * 64:(h + 1) * 64], True, True) for h in range(8)],
                     ['MT', 'Xb'], [('ps', by)])
                byo = alloc1()
                S.mm([(bank(byo)[:, g * 256:(g + 1) * 256], xbcT3[:, 6 + g, bc0:bc0 + 128], prevTb[:, g * 256:(g + 1) * 256], True, True) for g in range(2)],
                     [('xbcT', 6), ('xbcT', 7), 'prevTb'], [('ps', byo)])
                S.tt('dve', t1.rearrange("p (h e) -> p h e", h=8), bank(byo).rearrange("p (h e) -> p h e", h=8),
                     ecs[:, 0:8].unsqueeze(2).to_broadcast([128, 8, 64]), ALU.mult, [('ps', byo), ('ecs', par)], ['t1'])
                free1(byo)
                S.tt('dve', t1, t1, bank(by), ALU.add, ['t1', ('ps', by)], ['t1'])
                free1(by)
                bst = alloc1()
                S.mm([(bank(bst)[:, g * 256:(g + 1) * 256], B_tm[:, par * 256 + g * 128:par * 256 + (g + 1) * 128], Xwb[:, g * 256:(g + 1) * 256], True, True) for g in range(2)],
                     [('B_tm', par), 'Xwb'], [('ps', bst)])
                S.tt('dve', prevT.rearrange("p (h e) -> p h e", h=8), prevT.rearrange("p (h e) -> p h e", h=8),
                     ecs[:, 16:24].unsqueeze(2).to_broadcast([128, 8, 64]), ALU.mult, ['prevT', ('ecs', par)], ['prevT'])
                S.tt('dve', prevT, prevT, bank(bst), ALU.add, ['prevT', ('ps', bst)], ['prevT'])
                free1(bst)
                S.cp('act', prevTb, prevT, ['prevT'], ['prevTb'])
                if blk < 3:
                    ssd_pre(blk + 1)
                yield
                S.tt('dve', t1, t1, t2, ALU.add, ['t1', 't2'], ['t1'])
                S.tt('dve', t1, t1, sz[:, blk * 512:(blk + 1) * 512], ALU.mult, ['t1', ('sz', blk)], ['t1'])
                if 'noaccum' in flags:
                    S.act(t2, t1, AF.Square, ['t1'], ['t2'])
                    S.op('dve', lambda e: e.reduce_sum(out=ssq[:, 0:1], in_=t2, axis=AX.X), ['t2'], ['ssq'])
                else:
                    S.memset('pool', ssq[:, 0:1], 0.0, ['ssq'])
                    S.op('act', lambda e: e.activation(out=t2, in_=t1, func=AF.Square, accum_out=ssq[:, 0:1]), ['t1', 'ssq'], ['t2', 'ssq'])
                S.act(ssq[:, 0:1], ssq[:, 0:1], AF.Ln, ['ssq'], ['ssq'], bias=EPS, scale=1.0 / 512)
                S.act(ssq[:, 0:1], ssq[:, 0:1], AF.Exp, ['ssq'], ['ssq'], scale=-0.5)
                S.stt('dve', ym[:, 0:512], t1, ssq[:, 0:1], ssdnw, ALU.mult, ALU.mult, ['t1', 'ssq', 'ssdnw'], [('ymix', par)])
                yield

            def att_gen(t, blk):
                i = 4 * t + blk
                bc0 = blk * 128
                par = blk % 2
                ym = ymix[:, par * 1024:(par + 1) * 1024]
                s0 = max(0, 4 - i)
                OB = 4

                def scores(h):
                    hc, pbase = h // 2, (h % 2) * 64
                    pp = h % 2
                    sc = ps[:, pp * 1024:(pp + 1) * 1024]
                    lst = []
                    kkeys = []
                    if s0 < 4:
                        lst.append((sc[:, s0 * 128:512], identb, biasT3[:, h, s0 * 128:512], True, False))
                    lst.append((sc[:, 512:640], identb, biasT3[:, h, 512:640], True, False))
                    for s in range(s0, 5):
                        ks = (i - 4 + s) % 8
                        kkeys.append(('kT', ks))
                        lst.append((sc[:, s * 128:(s + 1) * 128], kT3[pbase:pbase + 64, hc, ks * 128:(ks + 1) * 128],
                                    qT3[pbase:pbase + 64, hc, bc0:bc0 + 128], False, True))
                    S.mm(lst, kkeys + ['qT', 'biasT'], [('ps', 2 * pp), ('ps', 2 * pp + 1)])
                    pt = PT[:, pp * 640:(pp + 1) * 640]
                    S.act(pt[:, s0 * 128:640], sc[:, s0 * 128:640], AF.Exp, [('ps', 2 * pp), ('ps', 2 * pp + 1)], [('PT', pp)])

                def pv(h):
                    pp = h % 2
                    pt = PT[:, pp * 640:(pp + 1) * 640]
                    o = bank(OB)[:, (h % 4) * 65:(h % 4 + 1) * 65]
                    lst = []
                    vkeys = []
                    for s in range(s0, 5):
                        ks = (i - 4 + s) % 8
                        vkeys.append(('v1', ks))
                        lst.append((o, pt[:, s * 128:(s + 1) * 128], v1v[:, ks, h, :], s == s0, s == 4))
                    S.mm(lst, vkeys + [('PT', pp)], [('ps', OB)])
                    if h % 4 == 3:
                        o3 = bank(OB)[:, 0:260].rearrange("p (h e) -> p h e", h=4)
                        r4 = rec[:, (h // 4) * 4:(h // 4) * 4 + 4]
                        S.op('dve', lambda e, r4=r4, o3=o3: e.reciprocal(out=r4, in_=o3[:, :, 64]), [('ps', OB)], ['rec'])
                        S.tt('dve', ym[:, 512 + (h // 4) * 256:512 + (h // 4 + 1) * 256].rearrange("p (h e) -> p h e", h=4), o3[:, :, 0:64],
                             r4.unsqueeze(2).to_broadcast([128, 4, 64]), ALU.mult, [('ps', OB), 'rec'], [('ymix', par)])

                scores(0)
                for h in range(1, 8):
                    scores(h)
                    if h % 2 == 1:
                        yield
                    pv(h - 1)
                yield
                pv(7)
                if 'ssd' not in flags:
                    S.memset('dve', ym[:, 0:512], 0.0, [('ymix', par)])
                if 'att' not in flags:
                    S.memset('dve', ym[:, 512:1024], 0.0, [('ymix', par)])
                bt = alloc1()
                pt16 = bank(bt).bitcast(BF16)
                S.tr([(pt16[:, c * 128:(c + 1) * 128], ym[:, c * 128:(c + 1) * 128], identb) for c in range(8)], [('ymix', par)], [('ps', bt)])
                S.cp('act', yT3[:, :, bc0:bc0 + 128], pt16.rearrange("p (c n) -> p c n", c=8), [('ps', bt)], ['yT'])
                free1(bt)
                yield

            def drive(gens):
                gens = list(gens)
                while gens:
                    for g_ in list(gens):
                        try:
                            next(g_)
                        except StopIteration:
                            gens.remove(g_)

            if not hoisted.pop('mix', False):
                norm_tile(0, lambda kc: Am4[:, l, kc, b:b + 1], lambda kc: mod4[:, l, kc, b:b + 1])
            for t in range(NTILE):
                c0 = t * TT
                projections(t)
                pump(8 if b == 0 else 0)
                for step in range(5):
                    gens = []
                    if step < 4:
                        gens.append(ssd_gen(t, step))
                    if step >= 1:
                        gens.append(att_gen(t, step - 1))
                    drive(gens)
                if t == 0:
                    dbgdump('yT', yT, ['yT'])
                if t + 1 < NTILE:
                    norm_tile(t + 1, lambda kc: Am4[:, l, kc, b:b + 1], lambda kc: mod4[:, l, kc, b:b + 1])
                elif 'ffn' in flags:
                    norm_tile(0, lambda kc: Af4[:, l, kc, b:b + 1], lambda kc: mod4[:, l, 24 + kc, b:b + 1])
                    hoisted['ffn'] = True
                for half in range(4):
                    slot, w3 = load_ring(wout_v, half * 256, wout_keys)
                    for oc in range(2):
                        m = half * 2 + oc
                        bo = alloc1()
                        S.mm([(bank(bo), w3[:, kc, oc * 128:(oc + 1) * 128], yT3[:, kc, :], kc == 0, kc == 7) for kc in range(8)],
                             ['yT', ('wring', slot)], [('ps', bo)])
                        S.stt('dve', xT3[:, m, c0:c0 + TT], bank(bo), mod4[:, l, 16 + m, b:b + 1], xT3[:, m, c0:c0 + TT], ALU.mult, ALU.add,
                              [('ps', bo), ('x', m, t)], [('x', m, t)])
                        free1(bo)
            if 'xmix' in flags:
                dbgdump('xmix', xT, [('x', kc_, t_) for kc_ in range(8) for t_ in range(NTILE)])
            S.barrier()

        def ffn_phase(b, l):
            AR.off = phase_base
            singles[0] = list(range(8))
            aT = AR.alloc(NJ * TT, BF16)
            aT3 = aT.rearrange("p (j n) -> p j n", j=NJ)
            wupr = [AR.alloc(8 * 2 * 256, BF16) for _ in range(3)]
            wdnr = [AR.alloc(NJ * 256, BF16), AR.alloc(NJ * 256, BF16)]
            ug = AR.alloc(2 * 516, F32)
            uv = AR.alloc(2 * 516, F32)
            tg = AR.alloc(2 * TT, F32)
            tv = AR.alloc(2 * TT, F32)
            accg = AR.alloc(2 * TT, F32)
            accv = AR.alloc(2 * TT, F32)
            sg = AR.alloc(2 * TT, F32)
            uhalo = AR.alloc(44 * 2, F32)
            uhalo3 = uhalo.rearrange("p (c k) -> p c k", c=44)
            fc4 = fconvw.rearrange("p (l c k) -> p l c k", l=NL, c=44)
            S.memset('pool', uhalo, 0.0, ['uhalo'])
            wup_v = wup_s[l].rearrange("(kc p) n -> p kc n", p=128)
            wdn_v = wdn_s[l].rearrange("(jc p) n -> p jc n", p=128)
            wup_keys = [('wup', l, kc) for kc in range(8)]
            wdn_keys = [('wdn', l, jc) for jc in range(NJ)]
            while not all(k_ in converted for k_ in wup_keys + wdn_keys):
                pump(1)
            do_ada = (b == 0 and l + 1 < nlayers)
            if do_ada:
                ada_slots = [AR.alloc(8 * 256, BF16), AR.alloc(8 * 256, BF16)]
            ada_ch = [0]
            if not hoisted.pop('ffn', False):
                norm_tile(0, lambda kc: Af4[:, l, kc, b:b + 1], lambda kc: mod4[:, l, 24 + kc, b:b + 1])
            for t in range(NTILE):
                c0 = t * TT
                for jp in range(NJ // 2):
                    slot = (t * (NJ // 2) + jp) % 3
                    wu = wupr[slot].rearrange("p (k a n) -> p k a n", k=8, a=2)
                    for a in range(2):
                        S.dma('sp', wu[:, :, a, :], wup_v[:, :, a * FFN + jp * 256:a * FFN + (jp + 1) * 256], reads=wup_keys, writes=[('wupr', slot)])
                    for jj in range(2):
                        jc = 2 * jp + jj
                        par = jc % 2
                        ps_ = slice(par * TT, (par + 1) * TT)
                        raw = []
                        for a, ubuf, nm, tbuf, tnm in ((0, ug, 'ug', tg, 'tg'), (1, uv, 'uv', tv, 'tv')):
                            ch = a * NJ + jc
                            bu = nb()
                            S.mm([(bank(bu), wu[:, kc, a, jj * 128:(jj + 1) * 128], hT3[:, kc, :], kc == 0, kc == 7) for kc in range(8)],
                                 ['hT', ('wupr', slot)], [('ps', bu)])
                            ur = ubuf[:, par * 516:par * 516 + 514]
                            S.cp('pool', ur[:, 0:2], uhalo3[:, ch, :], ['uhalo'], [(nm, par)])
                            S.act(ur[:, 2:514], bank(bu), AF.Copy, [('ps', bu)], [(nm, par)])
                            S.act(tbuf[:, ps_], bank(bu), AF.Copy, [('ps', bu)], [(tnm, par)], scale=fc4[:, l, ch, 2:3])
                            S.cp('pool', uhalo3[:, ch, :], ur[:, 512:514], [(nm, par)], ['uhalo'])
                            raw.append((ur, ch, nm))
                        (urg, chg, _), (urv, chv, _) = raw
                        S.stt('dve', accg[:, ps_], urg[:, 0:512], fc4[:, l, chg, 0:1], tg[:, ps_], ALU.mult, ALU.add, [('ug', par), ('tg', par)], [('accg', par)])
                        S.stt('dve', accg[:, ps_], urg[:, 1:513], fc4[:, l, chg, 1:2], accg[:, ps_], ALU.mult, ALU.add, [('ug', par), ('accg', par)], [('accg', par)])
                        S.act(sg[:, ps_], accg[:, ps_], AF.Silu, [('accg', par)], [('sg', par)])
                        S.stt('dve', accv[:, ps_], urv[:, 0:512], fc4[:, l, chv, 0:1], tv[:, ps_], ALU.mult, ALU.add, [('uv', par), ('tv', par)], [('accv', par)])
                        S.stt('dve', accv[:, ps_], urv[:, 1:513], fc4[:, l, chv, 1:2], accv[:, ps_], ALU.mult, ALU.add, [('uv', par), ('accv', par)], [('accv', par)])
                        S.tt('dve', aT3[:, jc, :], sg[:, ps_], accv[:, ps_], ALU.mult, [('sg', par), ('accv', par)], [('aT', jc)])
                    if do_ada and jp % 2 == 1:
                        ada_chunk(l + 1, ada_ch[0], ada_slots)
                        ada_ch[0] += 1
                if do_ada:
                    ada_chunk(l + 1, ada_ch[0], ada_slots)
                    ada_ch[0] += 1
                pump(8 if b == 0 else 0)
                if t + 1 < NTILE:
                    norm_tile(t + 1, lambda kc: Af4[:, l, kc, b:b + 1], lambda kc: mod4[:, l, 24 + kc, b:b + 1])
                elif l + 1 < nlayers and ('ssd' in flags or 'att' in flags):
                    if do_ada:
                        assert ada_ch[0] == 24, ada_ch[0]
                        ada_finish(l + 1)
                        ada_ch[0] = -1
                    norm_tile(0, lambda kc: Am4[:, l + 1, kc, b:b + 1], lambda kc: mod4[:, l + 1, kc, b:b + 1],
                              extra=[('Am', l + 1), ('mod', l + 1)])
                    hoisted['mix'] = True
                JS = 16
                for mp in range(4):
                    slot = mp % 2
                    wd = wdnr[slot].rearrange("p (j n) -> p j n", j=NJ)
                    S.dma('sp', wd, wdn_v[:, :, mp * 256:(mp + 1) * 256], reads=wdn_keys, writes=[('wdnr', slot)])
                    bos = [nb(), nb()]
                    if mp == 0:
                        for mm_ in range(2):
                            S.mm([(bank(bos[mm_]), wd[:, jc, mm_ * 128:(mm_ + 1) * 128], aT3[:, jc, :], jc == 0, False) for jc in range(JS)],
                                 [('aT', jc) for jc in range(JS)] + [('wdnr', slot)], [('ps', bos[mm_])])
                    for mm_ in range(2):
                        m = 2 * mp + mm_
                        bo = bos[mm_]
                        j0 = JS if mp == 0 else 0
                        S.mm([(bank(bo), wd[:, jc, mm_ * 128:(mm_ + 1) * 128], aT3[:, jc, :], jc == 0, jc == NJ - 1) for jc in range(j0, NJ)],
                             [('aT', jc) for jc in range(NJ)] + [('wdnr', slot)], [('ps', bo)])
                        S.stt('dve', xT3[:, m, c0:c0 + TT], bank(bo), mod4[:, l, 40 + m, b:b + 1], xT3[:, m, c0:c0 + TT], ALU.mult, ALU.add,
                              [('ps', bo), ('x', m, t)], [('x', m, t)])
            if do_ada and ada_ch[0] != -1:
                assert ada_ch[0] == 24, ada_ch[0]
                ada_finish(l + 1)
            S.barrier()

        for b in range(nseq):
            if b > 0:
                for kc in range(8):
                    S.dma('sp', xT3[:, kc, :], x_t[b, kc * 128:(kc + 1) * 128, :], writes=[('x', kc, t) for t in range(NTILE)])
            for l in range(nlayers):
                if 'ssd' in flags or 'att' in flags:
                    mixer_phase(b, l)
                if 'ffn' in flags:
                    ffn_phase(b, l)
            AR.off = phase_base
            ost = [AR.alloc(TT, F32), AR.alloc(TT, F32)]
            oi = 0
            for t in range(NTILE):
                c0 = t * TT
                bb = nb()
                for kc in range(8):
                    sl = sq[:, (kc % 2) * TT:(kc % 2 + 1) * TT]
                    S.act(sl, xT3[:, kc, c0:c0 + TT], AF.Square, [('x', kc, t)], [('sq', kc % 2)])
                    S.mm([(bank(bb), onesb, sl, kc == 0, kc == 7)], [('sq', kc % 2)], [('ps', bb)])
                S.act(rstd, bank(bb), AF.Ln, [('ps', bb)], ['rstd'], bias=EPS, scale=1.0 / D)
                S.act(rstd, rstd, AF.Exp, ['rstd'], ['rstd'], scale=-0.5)
                for kc in range(8):
                    o = ost[oi % 2]
                    S.stt('dve', o, xT3[:, kc, c0:c0 + TT], fnw[:, kc:kc + 1], rstd, ALU.mult, ALU.mult, [('x', kc, t), 'rstd'], [('ost', oi % 2)])
                    S.dma('sp', out_t[b, kc * 128:(kc + 1) * 128, c0:c0 + TT], o, reads=[('ost', oi % 2)], writes=[('out', b, kc, t)])
                    oi += 1
            S.barrier()
        S.finish([])
        print("sched: ops=%d waits=%d" % (S.nops, S.nwaits))
    return nc


def _shared_inputs(inp):
    L = NL
    f = np.float32
    k = np.arange(128)[:, None, None]
    s = np.arange(5)[None, :, None]
    q = np.arange(128)[None, None, :]
    dist = (4 - s) * 128 + q - k
    idx = np.clip(dist, -128, 128) + 128
    dc = 8 - 2 * s + q // 64 - k // 64
    valid = (dc >= 0) & (dc <= 8)
    g = np.asarray(inp["rel_bias"], f)[:, :, idx]
    g = np.where(valid[None, None], g, f(NEG))
    biasT = np.ascontiguousarray(g.transpose(0, 2, 1, 3, 4)).reshape(L, 128, 8 * 640)
    ones = np.ones((128, 128), f)
    sh = {
        "w_ada": np.ascontiguousarray(inp["w_ada"], f),
        "b_ada_t": np.ascontiguousarray(np.asarray(inp["b_ada"], f).reshape(L, 48, 128).transpose(2, 0, 1)).reshape(128, L * 48),
        "w_in": np.ascontiguousarray(inp["w_in"], f),
        "w_out": np.ascontiguousarray(inp["w_out"], f),
        "w_up": np.ascontiguousarray(inp["w_up"], f),
        "w_down": np.ascontiguousarray(inp["w_down"], f),
        "nmw_t": np.ascontiguousarray(np.asarray(inp["norm_mix_w"], f).reshape(L, 8, 128).transpose(2, 0, 1)).reshape(128, L * 8),
        "nfw_t": np.ascontiguousarray(np.asarray(inp["norm_ffn_w"], f).reshape(L, 8, 128).transpose(2, 0, 1)).reshape(128, L * 8),
        "fnw_t": np.ascontiguousarray(np.asarray(inp["final_norm_w"], f).reshape(8, 128).T),
        "convw_t": np.ascontiguousarray(np.asarray(inp["ssd_conv_w"], f).reshape(L, 4, 8, 128).transpose(3, 0, 2, 1)).reshape(128, L * 32),
        "convb_t": np.ascontiguousarray(np.asarray(inp["ssd_conv_b"], f).reshape(L, 8, 128).transpose(2, 0, 1)).reshape(128, L * 8),
        "fconvw_t": np.ascontiguousarray(np.asarray(inp["ffn_conv_w"], f).reshape(L, 3, 44, 128).transpose(3, 0, 2, 1)).reshape(128, L * 132),
        "dtb_bc": np.ascontiguousarray(np.broadcast_to(np.asarray(inp["dt_bias"], f).reshape(1, L * 8), (128, L * 8))),
        "alog_bc": np.ascontiguousarray(np.broadcast_to(np.asarray(inp["a_log"], f).reshape(1, L * 8), (128, L * 8))),
        "dskx": np.ascontiguousarray(np.broadcast_to(np.repeat(np.asarray(inp["d_skip"], f), 64, axis=1)[:, None, :], (L, 128, 512))),
        "ssdnw": np.ascontiguousarray(np.broadcast_to(np.asarray(inp["ssd_norm_w"], f)[:, None, :], (L, 128, 512))),
        "biasT": biasT,
        "identf": np.eye(128, dtype=f),
        "u1": np.ascontiguousarray(np.tril(ones, -1)),
        "u2": np.ascontiguousarray(np.triu(ones)),
        "mask01": np.ascontiguousarray(np.triu(ones)),
    }
    return sh


def _core_inputs(inp, sh, b0, nseq):
    f = np.float32
    m = dict(sh)
    m["x_t"] = np.ascontiguousarray(np.asarray(inp["x"][b0:b0 + nseq], f).transpose(0, 2, 1))
    m["c_t"] = np.ascontiguousarray(np.asarray(inp["c"][b0:b0 + nseq], f).reshape(nseq, 8, 128).transpose(2, 1, 0)).reshape(128, 8 * nseq)
    return m


_NC_CACHE = {}


def kernel(**inputs):
    n = 8
    nseq = 2
    sh = _shared_inputs(inputs)
    in_maps = [_core_inputs(inputs, sh, c * nseq, nseq) for c in range(n)]
    if "nc" not in _NC_CACHE:
        _NC_CACHE["nc"] = build_program(NL, nseq)
    res = run_bass_kernel_spmd(_NC_CACHE["nc"], in_maps, core_ids=list(range(n)))
    outs = [np.asarray(r["out_t"]).transpose(0, 2, 1) for r in res.results]
    return np.ascontiguousarray(np.concatenate(outs, axis=0).astype(np.float32))
```

```python
import numpy as np
import concourse.bass as bass
import concourse.mybir as mybir
from concourse.bass_utils import run_bass_kernel_spmd
from contextlib import ExitStack

F32 = mybir.dt.float32
BF16 = mybir.dt.bfloat16
AF = mybir.ActivationFunctionType
ALU = mybir.AluOpType
AX = mybir.AxisListType


class Sched:
    def __init__(self, nc, es, ring=8):
        self.nc = nc
        self.eobj = {'pe': nc.tensor, 'act': nc.scalar, 'dve': nc.vector, 'pool': nc.gpsimd, 'sp': nc.sync}
        self.prog = {k: [] for k in self.eobj}
        self.sem = {k: es.enter_context(nc.semaphore("s_" + k)) for k in ('pe', 'act', 'dve', 'pool')}
        self.cnt = {k: 0 for k in self.sem}
        self.ring = {q: [es.enter_context(nc.semaphore("d_%s%d" % (q, i))) for i in range(ring)] for q in ('sp', 'pool')}
        self.ringval = {q: [0] * ring for q in self.ring}
        self.ringidx = {q: 0 for q in self.ring}
        self.known = {k: {} for k in self.eobj}
        self.snap = {}
        self.last_w = {}
        self.readers = {}
        self.nwaits = 0
        self.nops = 0

    def _deps(self, reads, writes):
        toks = {}
        for k in reads:
            w = self.last_w.get(k)
            if w is not None:
                toks[(w[0], w[2])] = w
        for k in writes:
            w = self.last_w.get(k)
            if w is not None:
                toks[(w[0], w[2])] = w
            for r in self.readers.get(k, ()):
                toks[(r[0], r[2])] = r
        return sorted(toks.values(), key=lambda t: -t[2])

    def _wait(self, eng, tok):
        name, sem, val = tok
        kn = self.known[eng]
        if kn.get(name, 0) >= val:
            return
        if eng == 'pe' and name == 'pe':
            return
        self.prog[eng].append(lambda E, sem=sem, val=val: E.wait_ge(sem, val))
        self.nwaits += 1
        kn[name] = val
        sn = self.snap.get((name, val))
        if sn:
            for n2, v2 in sn.items():
                if kn.get(n2, 0) < v2:
                    kn[n2] = v2

    def _record(self, tok, reads, writes):
        for k in reads:
            self.readers.setdefault(k, []).append(tok)
        for k in writes:
            self.last_w[k] = tok
            self.readers[k] = []

    def op(self, eng, fn, reads=(), writes=()):
        for tok in self._deps(reads, writes):
            self._wait(eng, tok)
        self.cnt[eng] += 1
        v = self.cnt[eng]
        sem = self.sem[eng]
        self.prog[eng].append(lambda E, fn=fn, sem=sem: fn(E).then_inc(sem, 1))
        tok = (eng, sem, v)
        self.snap[(eng, v)] = dict(self.known[eng])
        self._record(tok, reads, writes)
        self.nops += 1
        return tok

    def dma(self, q, out, in_, reads=(), writes=()):
        for tok in self._deps(reads, writes):
            self._wait(q, tok)
        j = self.ringidx[q]
        self.ringidx[q] = (j + 1) % len(self.ring[q])
        sem = self.ring[q][j]
        name = "%s_ring%d" % (q, j)
        prev = self.ringval[q][j]
        if prev:
            self._wait(q, (name, sem, prev))
        val = prev + 16
        self.ringval[q][j] = val
        self.prog[q].append(lambda E, out=out, in_=in_, sem=sem: E.dma_start(out=out, in_=in_).then_inc(sem, 16))
        tok = (name, sem, val)
        self._record(tok, reads, writes)
        return tok

    def finish(self, keys, eng='sp'):
        for k in keys:
            w = self.last_w.get(k)
            if w is not None:
                self._wait(eng, w)
        nc = self.nc
        with nc.Block() as block:
            for name, deco in (('sp', block.sync), ('pe', block.tensor), ('act', block.scalar),
                               ('dve', block.vector), ('pool', block.gpsimd)):
                lst = self.prog[name]

                def body(E, lst=lst):
                    for f in lst:
                        f(E)
                deco(body)

    def barrier(self):
        toks = [(k, self.sem[k], self.cnt[k]) for k in self.sem if self.cnt[k] > 0]
        for q in self.ring:
            for j, sem in enumerate(self.ring[q]):
                if self.ringval[q][j]:
                    toks.append(("%s_ring%d" % (q, j), sem, self.ringval[q][j]))
        for eng in self.eobj:
            for tok in toks:
                self._wait(eng, tok)
        self.last_w = {}
        self.readers = {}

    def tt(self, eng, out, in0, in1, op, r, w):
        return self.op(eng, lambda e: e.tensor_tensor(out=out, in0=in0, in1=in1, op=op), r, w)

    def ts(self, eng, out, in0, s1, s2, op0, op1, r, w):
        if s2 is None:
            return self.op(eng, lambda e: e.tensor_scalar(out=out, in0=in0, scalar1=s1, scalar2=None, op0=op0), r, w)
        return self.op(eng, lambda e: e.tensor_scalar(out=out, in0=in0, scalar1=s1, scalar2=s2, op0=op0, op1=op1), r, w)

    def stt(self, eng, out, in0, scalar, in1, op0, op1, r, w):
        return self.op(eng, lambda e: e.scalar_tensor_tensor(out=out, in0=in0, scalar=scalar, in1=in1, op0=op0, op1=op1), r, w)

    def act(self, out, in_, func, r, w, bias=None, scale=None):
        kw = {}
        if bias is not None:
            kw['bias'] = bias
        if scale is not None:
            kw['scale'] = scale
        return self.op('act', lambda e: e.activation(out=out, in_=in_, func=func, **kw), r, w)

    def cp(self, eng, out, in_, r, w):
        if eng == 'act':
            return self.op(eng, lambda e: e.activation(out=out, in_=in_, func=AF.Copy), r, w)
        return self.op(eng, lambda e: e.tensor_copy(out=out, in_=in_), r, w)

    def mm(self, lst, r, w):
        def f(e):
            for (o, l, rh, st, sp) in lst:
                i = e.matmul(o, lhsT=l, rhs=rh, start=st, stop=sp)
            return i
        return self.op('pe', f, r, w)

    def tr(self, lst, r, w):
        def f(e):
            for (o, i_, idn) in lst:
                i = e.transpose(o, i_, idn)
            return i
        return self.op('pe', f, r, w)

    def memset(self, eng, ap, val, w):
        return self.op(eng, lambda e: e.memset(ap, val), (), w)


class Arena:
    def __init__(self, nc, es, nbytes):
        self.t = es.enter_context(nc.sbuf_tensor("arena", [128, nbytes // 4], F32))
        self.nbytes = nbytes
        self.off = 0

    def alloc(self, n, dt):
        sz = n * (4 if dt == F32 else 2)
        sz = (sz + 31) // 32 * 32
        assert self.off + sz <= self.nbytes, ("arena overflow", self.off, sz, self.nbytes)
        ap = self.t[:, self.off // 4:(self.off + sz) // 4]
        if dt != F32:
            ap = ap.bitcast(dt)
        self.off += sz
        return ap[:, 0:n]


D = 1024
SL = 2048
NL = 4
TT = 512
NTILE = SL // TT
IN_DIM = 3080
FFN = 2816
NJ = FFN // 128
EPS = 1e-6
NEG = -30000.0


def build_program(nlayers=NL, nseq=2, flags=('ssd', 'att', 'ffn')):
    nc = bass.Bass("TRN2", target_bir_lowering=False)

    def din(name, shape):
        return nc.dram_tensor(name, shape, F32, kind="ExternalInput").ap()

    x_t = din("x_t", [nseq, D, SL])
    c_t = din("c_t", [128, 8 * nseq])
    w_ada = din("w_ada", [NL, D, 6 * D])
    b_ada_t = din("b_ada_t", [128, NL * 48])
    w_in = din("w_in", [NL, D, IN_DIM])
    w_out = din("w_out", [NL, D, D])
    w_up = din("w_up", [NL, D, 2 * FFN])
    w_down = din("w_down", [NL, FFN, D])
    nmw_t = din("nmw_t", [128, NL * 8])
    nfw_t = din("nfw_t", [128, NL * 8])
    fnw_t = din("fnw_t", [128, 8])
    convw_t = din("convw_t", [128, NL * 32])
    convb_t = din("convb_t", [128, NL * 8])
    fconvw_t = din("fconvw_t", [128, NL * 132])
    dtb_bc = din("dtb_bc", [128, NL * 8])
    alog_bc = din("alog_bc", [128, NL * 8])
    dskx_d = din("dskx", [NL, 128, 512])
    ssdnw_d = din("ssdnw", [NL, 128, 512])
    biasT_d = din("biasT", [NL, 128, 8 * 640])
    identf_d = din("identf", [128, 128])
    u1_d = din("u1", [128, 128])
    u2_d = din("u2", [128, 128])
    mask01_d = din("mask01", [128, 128])
    out_t = nc.dram_tensor("out_t", [nseq, D, SL], F32, kind="ExternalOutput").ap()
    win_s = nc.dram_tensor("win_s", [NL, D, IN_DIM], BF16, kind="Internal").ap()
    wout_s = nc.dram_tensor("wout_s", [NL, D, D], BF16, kind="Internal").ap()
    wup_s = nc.dram_tensor("wup_s", [NL, D, 2 * FFN], BF16, kind="Internal").ap()
    wdn_s = nc.dram_tensor("wdn_s", [NL, FFN, D], BF16, kind="Internal").ap()

    with ExitStack() as es:
        S = Sched(nc, es)
        AR = Arena(nc, es, 207360)
        ps = es.enter_context(nc.psum_tensor("ps", [128, 4096], F32))
        dbg_done = set()

        def dbgdump(name, ap, keys):
            if 'dbg' not in flags or name in dbg_done:
                return
            dbg_done.add(name)
            d = nc.dram_tensor("dbg_" + name, list(ap.shape), ap.dtype, kind="ExternalOutput").ap()
            S.dma('sp', d, ap, reads=keys, writes=[('dbg', name)])

        def bank(i):
            return ps[:, i * 512:(i + 1) * 512]
        bk = [0]
        pk = [0]

        singles = [[6, 7]]

        def nb():
            lst = singles[0]
            i = bk[0] % len(lst)
            bk[0] = i + 1
            return lst[i]

        def npair():
            i = pk[0]
            pk[0] = (i + 1) % 2
            return i

        xT = AR.alloc(8 * SL, F32)
        xT3 = xT.rearrange("p (c n) -> p c n", c=8)
        identf = AR.alloc(128, F32)
        identb = AR.alloc(128, BF16)
        onesb = AR.alloc(128, BF16)
        onesf = AR.alloc(128, F32)
        u1 = AR.alloc(128, F32)
        u2 = AR.alloc(128, F32)
        mask01 = AR.alloc(128, F32)
        mod = AR.alloc(NL * 48 * nseq, F32)
        mod4 = mod.rearrange("p (l j b) -> p l j b", l=NL, j=48)
        Am = AR.alloc(NL * 8 * nseq, F32)
        Am4 = Am.rearrange("p (l k b) -> p l k b", l=NL, k=8)
        Af = AR.alloc(NL * 8 * nseq, F32)
        Af4 = Af.rearrange("p (l k b) -> p l k b", l=NL, k=8)
        cact = AR.alloc(8 * nseq, F32)
        cact3 = cact.rearrange("p (k b) -> p k b", k=8)
        bada = AR.alloc(NL * 48, F32)
        bada3 = bada.rearrange("p (l j) -> p l j", l=NL)
        nmw = AR.alloc(NL * 8, F32)
        nfw = AR.alloc(NL * 8, F32)
        fnw = AR.alloc(8, F32)
        convw = AR.alloc(NL * 32, F32)
        convb = AR.alloc(NL * 8, F32)
        fconvw = AR.alloc(NL * 132, F32)
        dtb = AR.alloc(NL * 8, F32)
        Aneg = AR.alloc(NL * 8, F32)
        hT = AR.alloc(8 * TT, BF16)
        hT3 = hT.rearrange("p (c n) -> p c n", c=8)
        sq = AR.alloc(2 * TT, BF16)
        hn = AR.alloc(2 * TT, F32)
        rstd = AR.alloc(TT, F32)
        persist_end = AR.off

        for dst, src, k in ((identf, identf_d, 'identf'), (u1, u1_d, 'u1'), (u2, u2_d, 'u2'), (mask01, mask01_d, 'mask01'),
                            (bada, b_ada_t, 'bada'), (nmw, nmw_t, 'nmw'), (nfw, nfw_t, 'nfw'), (fnw, fnw_t, 'fnw'),
                            (convw, convw_t, 'convw'), (convb, convb_t, 'convb'), (fconvw, fconvw_t, 'fconvw'),
                            (dtb, dtb_bc, 'dtb'), (Aneg, alog_bc, 'Aneg'), (cact, c_t, 'cact')):
            S.dma('sp', dst, src, writes=[k])
        S.cp('dve', identb, identf, ['identf'], ['identb'])
        S.memset('dve', onesb, 1.0, ['onesb'])
        S.memset('dve', onesf, 1.0, ['onesf'])
        S.act(Aneg, Aneg, AF.Exp, ['Aneg'], ['Aneg'])
        S.ts('dve', Aneg, Aneg, -1.0, None, ALU.mult, None, ['Aneg'], ['Aneg'])
        S.act(cact, cact, AF.Silu, ['cact'], ['cact'])

        def conv_items(l):
            items = []
            for kc in range(8):
                items.append((win_s[l, kc * 128:(kc + 1) * 128, :], w_in[l, kc * 128:(kc + 1) * 128, :], ('win', l, kc)))
            for kc in range(8):
                items.append((wout_s[l, kc * 128:(kc + 1) * 128, :], w_out[l, kc * 128:(kc + 1) * 128, :], ('wout', l, kc)))
            for kc in range(8):
                items.append((wup_s[l, kc * 128:(kc + 1) * 128, :], w_up[l, kc * 128:(kc + 1) * 128, :], ('wup', l, kc)))
            for jc in range(NJ):
                items.append((wdn_s[l, jc * 128:(jc + 1) * 128, :], w_down[l, jc * 128:(jc + 1) * 128, :], ('wdn', l, jc)))
            return items
        pending = []
        converted = set()
        hoisted = {}

        def pump(n):
            for _ in range(n):
                if not pending:
                    return
                o, i_, k = pending.pop(0)
                S.dma('pool', o, i_, writes=[k])
                converted.add(k)
        for kc in range(8):
            S.dma('sp', xT3[:, kc, :], x_t[0, kc * 128:(kc + 1) * 128, :], writes=[('x', kc, t) for t in range(NTILE)])
        for l_ in range(nlayers):
            pending.extend(conv_items(l_))
        pump(16)

        cactb = AR.alloc(8 * nseq, BF16)
        cactb3 = cactb.rearrange("p (k b) -> p k b", k=8)
        S.cp('dve', cactb, cact, ['cact'], ['cactb'])
        persist_end = AR.off

        def ada_chunk(l, ch, slots, fp32=False):
            wsrc = w_ada[l].rearrange("(kc p) n -> p kc n", p=128)
            slot = ch % 2
            w3 = slots[slot].rearrange("p (k n) -> p k n", k=8)
            S.dma('sp' if fp32 else 'pool', w3, wsrc[:, :, ch * 256:(ch + 1) * 256], writes=[('wada', slot)])
            b_ = nb()
            lst = []
            rhs3 = cact3 if fp32 else cactb3
            for jj in range(2):
                for kc in range(8):
                    lst.append((bank(b_)[:, jj * nseq:(jj + 1) * nseq], w3[:, kc, jj * 128:(jj + 1) * 128], rhs3[:, kc, :], kc == 0, kc == 7))
            S.mm(lst, [('wada', slot), 'cactb', 'cact'], [('ps', b_)])
            S.tt('dve', mod4[:, l, ch * 2:(ch + 1) * 2, :], bank(b_)[:, 0:2 * nseq].rearrange("p (j b) -> p j b", j=2),
                 bada3[:, l, ch * 2:(ch + 1) * 2].unsqueeze(2).to_broadcast([128, 2, nseq]), ALU.add, [('ps', b_), 'bada'], [('mod', l)])

        def ada_finish(l):
            nm3 = nmw.rearrange("p (l k) -> p l k", l=NL)
            nf3 = nfw.rearrange("p (l k) -> p l k", l=NL)
            S.stt('dve', Am4[:, l], mod4[:, l, 8:16, :], 1.0, nm3[:, l, :].unsqueeze(2).to_broadcast([128, 8, nseq]), ALU.add, ALU.mult,
                  [('mod', l), 'nmw'], [('Am', l)])
            S.stt('dve', Af4[:, l], mod4[:, l, 32:40, :], 1.0, nf3[:, l, :].unsqueeze(2).to_broadcast([128, 8, nseq]), ALU.add, ALU.mult,
                  [('mod', l), 'nfw'], [('Af', l)])

        ada_mark = AR.off
        slots0 = [AR.alloc(8 * 256, F32), AR.alloc(8 * 256, F32)]
        for ch in range(24):
            ada_chunk(0, ch, slots0, fp32=True)
        ada_finish(0)
        S.barrier()
        AR.off = ada_mark
        phase_base = AR.off

        def norm_tile(t, Aap, shap, extra=()):
            extra = list(extra)
            c0 = t * TT
            b = nb()
            for kc in range(8):
                sl = sq[:, (kc % 2) * TT:(kc % 2 + 1) * TT]
                S.act(sl, xT3[:, kc, c0:c0 + TT], AF.Square, [('x', kc, t)], [('sq', kc % 2)])
                S.mm([(bank(b), onesb, sl, kc == 0, kc == 7)], [('sq', kc % 2)], [('ps', b)])
            S.act(rstd, bank(b), AF.Ln, [('ps', b)], ['rstd'], bias=EPS, scale=1.0 / D)
            S.act(rstd, rstd, AF.Exp, ['rstd'], ['rstd'], scale=-0.5)
            for kc in range(8):
                hs = hn[:, (kc % 2) * TT:(kc % 2 + 1) * TT]
                S.stt('dve', hs, xT3[:, kc, c0:c0 + TT], Aap(kc), rstd, ALU.mult, ALU.mult, [('x', kc, t), 'rstd'] + extra, [('hn', kc % 2)])
                S.act(hT3[:, kc, :], hs, AF.Identity, [('hn', kc % 2)] + extra, ['hT'], bias=shap(kc))

        def mixer_phase(b, l):
            AR.off = phase_base
            singles[0] = [5, 6, 7]
            kT = AR.alloc(4 * 1024, BF16)
            kT3 = kT.rearrange("p (c n) -> p c n", c=4)
            v1 = AR.alloc(8 * 8 * 65, BF16)
            v1v = v1.rearrange("p (s h e) -> p s h e", s=8, h=8)
            biasT = AR.alloc(8 * 640, BF16)
            biasT3 = biasT.rearrange("p (h n) -> p h n", h=8)
            wdtb = AR.alloc(64, BF16)
            wdtb3 = wdtb.rearrange("p (k n) -> p k n", k=8)
            wring = [AR.alloc(8 * 256, BF16) for _ in range(4)]
            qT = AR.alloc(4 * TT, BF16)
            qT3 = qT.rearrange("p (c n) -> p c n", c=4)
            xraw = AR.alloc(2 * 516, F32)
            halo = AR.alloc(8 * 3, F32)
            halo3 = halo.rearrange("p (c k) -> p c k", c=8)
            cacc = AR.alloc(2 * TT, F32)
            xbcT = AR.alloc(8 * TT, BF16)
            xbcT3 = xbcT.rearrange("p (c n) -> p c n", c=8)
            xs_tm = AR.alloc(2 * 512, BF16)
            B_tm = AR.alloc(2 * 256, BF16)
            sz = AR.alloc(4 * 512, BF16)
            dtt = AR.alloc(4 * 8, F32)
            adt = AR.alloc(4 * 8, F32)
            dtmp = AR.alloc(32, F32)
            rhsU2x = [AR.alloc(1024, F32), AR.alloc(1024, F32)]
            DT = AR.alloc(1024, F32)
            CBm = AR.alloc(256, F32)
            MT = AR.alloc(1024, BF16)
            Xb = AR.alloc(512, BF16)
            Xwb = AR.alloc(512, BF16)
            ecsx = AR.alloc(48, F32)
            wdt = AR.alloc(8, F32)
            t1 = AR.alloc(512, F32)
            t2 = AR.alloc(512, F32)
            prevT = AR.alloc(512, F32)
            prevTb = AR.alloc(512, BF16)
            ssq = AR.alloc(2, F32)
            PT = AR.alloc(2 * 640, BF16)
            rec = AR.alloc(8, F32)
            ymix = AR.alloc(2 * 1024, BF16)
            yT = AR.alloc(8 * TT, BF16)
            yT3 = yT.rearrange("p (c n) -> p c n", c=8)
            dskx = AR.alloc(512, F32)
            ssdnw = AR.alloc(512, F32)

            S.dma('pool', biasT, biasT_d[l], writes=['biasT'])
            S.dma('sp', dskx, dskx_d[l], writes=['dskx'])
            S.dma('sp', ssdnw, ssdnw_d[l], writes=['ssdnw'])
            win_v = win_s[l].rearrange("(kc p) n -> p kc n", p=128)
            wout_v = wout_s[l].rearrange("(kc p) n -> p kc n", p=128)
            win_keys = [('win', l, kc) for kc in range(8)]
            wout_keys = [('wout', l, kc) for kc in range(8)]
            while not all(k_ in converted for k_ in win_keys + wout_keys):
                pump(1)
            S.dma('sp', wdtb3, win_v[:, :, 1536:1544], reads=win_keys, writes=['wdtb'])
            S.memset('pool', v1v[:, :, :, 64:65], 1.0, ['v1'])
            S.memset('pool', prevT, 0.0, ['prevT'])
            S.memset('pool', prevTb, 0.0, ['prevTb'])
            S.memset('pool', halo, 0.0, ['halo'])
            convw4 = convw.rearrange("p (l c k) -> p l c k", l=NL, c=8)
            convb3 = convb.rearrange("p (l c) -> p l c", l=NL)
            dtb3 = dtb.rearrange("p (l h) -> p l h", l=NL)
            Aneg3 = Aneg.rearrange("p (l h) -> p l h", l=NL)
            rs = [0]

            def load_ring(src_v, c0, rkeys):
                slot = rs[0] % 4
                rs[0] += 1
                w3 = wring[slot].rearrange("p (k n) -> p k n", k=8)
                S.dma('sp', w3, src_v[:, :, c0:c0 + 256], reads=rkeys, writes=[('wring', slot)])
                return slot, w3

            free_banks = [5, 6, 7]

            def alloc1():
                assert free_banks, "out of PSUM banks"
                return free_banks.pop(0)

            def free1(bi):
                free_banks.append(bi)

            def proj_fm(w3, slot, oc):
                bq = alloc1()
                S.mm([(bank(bq), w3[:, kc, oc * 128:(oc + 1) * 128], hT3[:, kc, :], kc == 0, kc == 7) for kc in range(8)],
                     ['hT', ('wring', slot)], [('ps', bq)])
                return bq

            def proj_tm(w3, slot, blk):
                bq = alloc1()
                S.mm([(bank(bq)[:, 0:256], hT3[:, kc, blk * 128:(blk + 1) * 128], w3[:, kc, :], kc == 0, kc == 7) for kc in range(8)],
                     ['hT', ('wring', slot)], [('ps', bq)])
                return bq

            def projections(t):
                bd = alloc1()
                S.mm([(bank(bd)[:, blk * 8:(blk + 1) * 8], hT3[:, kc, blk * 128:(blk + 1) * 128], wdtb3[:, kc, :], kc == 0, kc == 7)
                      for blk in range(4) for kc in range(8)], ['hT', 'wdtb'], [('ps', bd)])
                dkeys = [('dt', blk) for blk in range(4)]
                akeys = [('adt', blk) for blk in range(4)]
                S.tt('dve', dtmp.rearrange("p (j h) -> p j h", j=4), bank(bd)[:, 0:32].rearrange("p (j h) -> p j h", j=4),
                     dtb3[:, l, :].unsqueeze(1).to_broadcast([128, 4, 8]), ALU.add, [('ps', bd)], ['dtmp'])
                free1(bd)
                S.act(dtmp, dtmp, AF.Exp, ['dtmp'], ['dtmp'])
                S.act(dtt, dtmp, AF.Ln, ['dtmp'], dkeys, bias=1.0)
                S.tt('dve', adt.rearrange("p (j h) -> p j h", j=4), dtt.rearrange("p (j h) -> p j h", j=4),
                     Aneg3[:, l, :].unsqueeze(1).to_broadcast([128, 4, 8]), ALU.mult, dkeys, akeys)
                zs = [load_ring(win_v, 0, win_keys), load_ring(win_v, 256, win_keys)]
                for blk in range(4):
                    for hf in range(2):
                        slot, w3 = zs[hf]
                        bz = proj_tm(w3, slot, blk)
                        S.act(sz[:, blk * 512 + hf * 256:blk * 512 + (hf + 1) * 256], bank(bz)[:, 0:256], AF.Silu, [('ps', bz)], [('sz', blk)])
                        free1(bz)
                kbase = ((4 * t) % 8) * 128

                def conv_chunk(ci):
                    slot, w3 = load_ring(win_v, 512 + ci * 256, win_keys)
                    for oc in range(2):
                        c = ci * 2 + oc
                        bx = proj_fm(w3, slot, oc)
                        par = c % 2
                        xr = xraw[:, par * 516:par * 516 + 515]
                        ca = cacc[:, par * TT:(par + 1) * TT]
                        S.cp('pool', xr[:, 0:3], halo3[:, c, :], ['halo'], [('xraw', par)])
                        S.act(xr[:, 3:515], bank(bx), AF.Copy, [('ps', bx)], [('xraw', par)])
                        S.act(ca, bank(bx), AF.Copy, [('ps', bx)], [('cacc', par)], scale=convw4[:, l, c, 3:4])
                        free1(bx)
                        S.cp('pool', halo3[:, c, :], xr[:, 512:515], [('xraw', par)], ['halo'])
                        for k in range(3):
                            S.stt('dve', ca, xr[:, k:k + 512], convw4[:, l, c, k:k + 1], ca, ALU.mult, ALU.add, [('xraw', par), ('cacc', par)], [('cacc', par)])
                        S.act(xbcT3[:, c, :], ca, AF.Silu, [('cacc', par)], [('xbcT', c)], bias=convb3[:, l, c:c + 1])

                def q_chunk(ci):
                    slot, w3 = load_ring(win_v, 1544 + ci * 256, win_keys)
                    for oc2 in range(2):
                        oc = ci * 2 + oc2
                        bq = proj_fm(w3, slot, oc2)
                        S.act(qT3[:, oc, :], bank(bq), AF.Copy, [('ps', bq)], ['qT'], scale=0.125)
                        free1(bq)

                def k_chunk(ci):
                    slot, w3 = load_ring(win_v, 2056 + ci * 256, win_keys)
                    for oc2 in range(2):
                        oc = ci * 2 + oc2
                        bq = proj_fm(w3, slot, oc2)
                        S.cp('dve', kT3[:, oc, kbase:kbase + 512], bank(bq), [('ps', bq)], [('kT', (4 * t + j) % 8) for j in range(4)])
                        free1(bq)

                conv_chunk(0)
                q_chunk(0)
                conv_chunk(1)
                q_chunk(1)
                conv_chunk(2)
                k_chunk(0)
                conv_chunk(3)
                k_chunk(1)
                vs = [load_ring(win_v, 2568, win_keys), load_ring(win_v, 2824, win_keys)]
                for blk in range(4):
                    ks = (4 * t + blk) % 8
                    for hf in range(2):
                        slot, w3 = vs[hf]
                        bv = proj_tm(w3, slot, blk)
                        S.cp('dve', v1v[:, ks, hf * 4:(hf + 1) * 4, 0:64], bank(bv)[:, 0:256].rearrange("p (h e) -> p h e", h=4), [('ps', bv), 'v1'], [('v1', ks)])
                        free1(bv)
                emit_rhsU2(0)
                ssd_pre(0)

            zv_w = {}

            def zv_gen(t, blk):
                for hf in range(2):
                    slot, w3 = zv_w['z'][hf]
                    bz = proj_tm(w3, slot, blk)
                    S.act(sz[:, blk * 512 + hf * 256:blk * 512 + (hf + 1) * 256], bank(bz)[:, 0:256], AF.Silu, [('ps', bz)], [('sz', blk)])
                    free1(bz)
                yield
                ks = (4 * t + blk) % 8
                for hf in range(2):
                    slot, w3 = zv_w['v'][hf]
                    bv = proj_tm(w3, slot, blk)
                    S.cp('dve', v1v[:, ks, hf * 4:(hf + 1) * 4, 0:64], bank(bv)[:, 0:256].rearrange("p (h e) -> p h e", h=4), [('ps', bv), 'v1'], [('v1', ks)])
                    free1(bv)
                yield

            def emit_rhsU2(blk):
                r_ = rhsU2x[blk % 2]
                S.tt('pool', r_.rearrange("p (h n) -> p h n", h=8), u2.unsqueeze(1).to_broadcast([128, 8, 128]),
                     adt[:, blk * 8:(blk + 1) * 8].unsqueeze(2).to_broadcast([128, 8, 128]), ALU.mult, [('adt', blk)], [('rhsU2', blk % 2)])

            def ssd_pre(blk):
                bc0 = blk * 128
                par = blk % 2
                rhsU2 = rhsU2x[par]
                ecs = ecsx[:, par * 24:(par + 1) * 24]
                btr = alloc1()
                pb16 = bank(btr).bitcast(BF16)
                S.tr([(pb16[:, c * 128:(c + 1) * 128], xbcT3[:, c, bc0:bc0 + 128], identb) for c in range(6)],
                     [('xbcT', c) for c in range(6)], [('ps', btr)])
                S.cp('act', xs_tm[:, par * 512:(par + 1) * 512], pb16[:, 0:512], [('ps', btr)], [('xs_tm', par)])
                S.cp('act', B_tm[:, par * 256:(par + 1) * 256], pb16[:, 512:768], [('ps', btr)], [('B_tm', par)])
                free1(btr)
                adtb = adt[:, blk * 8:(blk + 1) * 8]
                bs = alloc1()
                S.mm([(bank(bs)[:, 0:8], u2, adtb, True, True), (bank(bs)[:, 8:16], u1, adtb, True, True),
                      (bank(bs)[:, 16:24], onesf, adtb, True, True)], [('adt', blk)], [('ps', bs)])
                S.act(ecs, bank(bs)[:, 0:24], AF.Exp, [('ps', bs)], [('ecs', par)])
                free1(bs)
                sa, sb_ = alloc1(), alloc1()
                S.mm([(bank(sa), u1, rhsU2[:, 0:512], True, True), (bank(sb_), u1, rhsU2[:, 512:1024], True, True)],
                     [('rhsU2', par)], [('ps', sa), ('ps', sb_)])
                S.act(DT[:, 0:512], bank(sa), AF.Exp, [('ps', sa)], ['DT'])
                S.act(DT[:, 512:1024], bank(sb_), AF.Exp, [('ps', sb_)], ['DT'])
                free1(sa)
                free1(sb_)
                bcb = alloc1()
                S.mm([(bank(bcb)[:, g * 128:(g + 1) * 128], xbcT3[:, 4 + g, bc0:bc0 + 128], xbcT3[:, 6 + g, bc0:bc0 + 128], True, True) for g in range(2)],
                     [('xbcT', c) for c in range(4, 8)], [('ps', bcb)])
                S.tt('dve', CBm.rearrange("p (g n) -> p g n", g=2), bank(bcb)[:, 0:256].rearrange("p (g n) -> p g n", g=2),
                     mask01.unsqueeze(1).to_broadcast([128, 2, 128]), ALU.mult, [('ps', bcb)], ['CBm'])
                free1(bcb)

            def ssd_gen(t, blk):
                bc0 = blk * 128
                par = blk % 2
                ym = ymix[:, par * 1024:(par + 1) * 1024]
                ecs = ecsx[:, par * 24:(par + 1) * 24]
                xsb = xs_tm[:, par * 512:(par + 1) * 512]
                xsb3 = xsb.rearrange("p (h e) -> p h e", h=8)
                dtb_ = dtt[:, blk * 8:(blk + 1) * 8]
                S.tt('dve', MT.rearrange("p (g e n) -> p g e n", g=2, e=4), DT.rearrange("p (g e n) -> p g e n", g=2, e=4),
                     CBm.rearrange("p (g n) -> p g n", g=2).unsqueeze(2).to_broadcast([128, 2, 4, 128]), ALU.mult, ['DT', 'CBm'], ['MT'])
                S.tt('dve', wdt, ecs[:, 8:16], dtb_, ALU.mult, [('ecs', par), ('dt', blk)], ['wdt'])
                S.tt('pool', Xb.rearrange("p (h e) -> p h e", h=8), xsb3, dtb_.unsqueeze(2).to_broadcast([128, 8, 64]), ALU.mult,
                     [('xs_tm', par), ('dt', blk)], ['Xb'])
                S.tt('pool', Xwb.rearrange("p (h e) -> p h e", h=8), xsb3, wdt.unsqueeze(2).to_broadcast([128, 8, 64]), ALU.mult,
                     [('xs_tm', par), 'wdt'], ['Xwb'])
                S.tt('pool', t2, xsb, dskx, ALU.mult, [('xs_tm', par), 'dskx'], ['t2'])
                if blk < 3:
                    emit_rhsU2(blk + 1)
                yield
                by = alloc1()
                S.mm([(bank(by)[:, h * 64:(h + 1) * 64], MT[:, h * 128:(h + 1) * 128], Xb[:, h * 64:(h + 1) * 64], True, True) for h in range(8)],
                     ['MT', 'Xb'], [('ps', by)])
                byo = alloc1()
                S.mm([(bank(byo)[:, g * 256:(g + 1) * 256], xbcT3[:, 6 + g, bc0:bc0 + 128], prevTb[:, g * 256:(g + 1) * 256], True, True) for g in range(2)],
                     [('xbcT', 6), ('xbcT', 7), 'prevTb'], [('ps', byo)])
                S.tt('dve', t1.rearrange("p (h e) -> p h e", h=8), bank(byo).rearrange("p (h e) -> p h e", h=8),
                     ecs[:, 0:8].unsqueeze(2).to_broadcast([128, 8, 64]), ALU.mult, [('ps', byo), ('ecs', par)], ['t1'])
                free1(byo)
                S.tt('dve', t1, t1, bank(by), ALU.add, ['t1', ('ps', by)], ['t1'])
                free1(by)
                bst = alloc1()
                S.mm([(bank(bst)[:, g * 256:(g + 1) * 256], B_tm[:, par * 256 + g * 128:par * 256 + (g + 1) * 128], Xwb[:, g * 256:(g + 1) * 256], True, True) for g in range(2)],
                     [('B_tm', par), 'Xwb'], [('ps', bst)])
                S.tt('dve', prevT.rearrange("p (h e) -> p h e", h=8), prevT.rearrange("p (h e) -> p h e", h=8),
                     ecs[:, 16:24].unsqueeze(2).to_broadcast([128, 8, 64]), ALU.mult, ['prevT', ('ecs', par)], ['prevT'])
                S.tt('dve', prevT, prevT, bank(bst), ALU.add, ['prevT', ('ps', bst)], ['prevT'])
                free1(bst)
                S.cp('pool', prevTb, prevT, ['prevT'], ['prevTb'])
                if blk < 3:
                    ssd_pre(blk + 1)
                yield
                S.tt('dve', t1, t1, t2, ALU.add, ['t1', 't2'], ['t1'])
                S.tt('dve', t1, t1, sz[:, blk * 512:(blk + 1) * 512], ALU.mult, ['t1', ('sz', blk)], ['t1'])
                if 'noaccum' in flags:
                    S.act(t2, t1, AF.Square, ['t1'], ['t2'])
                    S.op('dve', lambda e: e.reduce_sum(out=ssq[:, 0:1], in_=t2, axis=AX.X), ['t2'], ['ssq'])
                else:
                    S.memset('pool', ssq[:, 0:1], 0.0, ['ssq'])
                    S.op('act', lambda e: e.activation(out=t2, in_=t1, func=AF.Square, accum_out=ssq[:, 0:1]), ['t1', 'ssq'], ['t2', 'ssq'])
                S.act(ssq[:, 0:1], ssq[:, 0:1], AF.Ln, ['ssq'], ['ssq'], bias=EPS, scale=1.0 / 512)
                S.act(ssq[:, 0:1], ssq[:, 0:1], AF.Exp, ['ssq'], ['ssq'], scale=-0.5)
                S.stt('dve', ym[:, 0:512], t1, ssq[:, 0:1], ssdnw, ALU.mult, ALU.mult, ['t1', 'ssq', 'ssdnw'], [('ymix', par)])
                yield

            def att_gen(t, blk):
                i = 4 * t + blk
                bc0 = blk * 128
                par = blk % 2
                ym = ymix[:, par * 1024:(par + 1) * 1024]
                s0 = max(0, 4 - i)
                OB = 4

                def scores(h):
                    hc, pbase = h // 2, (h % 2) * 64
                    pp = h % 2
                    sc = ps[:, pp * 1024:(pp + 1) * 1024]
                    lst = []
                    kkeys = []
                    if s0 < 4:
                        lst.append((sc[:, s0 * 128:512], identb, biasT3[:, h, s0 * 128:512], True, False))
                    lst.append((sc[:, 512:640], identb, biasT3[:, h, 512:640], True, False))
                    for s in range(s0, 5):
                        ks = (i - 4 + s) % 8
                        kkeys.append(('kT', ks))
                        lst.append((sc[:, s * 128:(s + 1) * 128], kT3[pbase:pbase + 64, hc, ks * 128:(ks + 1) * 128],
                                    qT3[pbase:pbase + 64, hc, bc0:bc0 + 128], False, True))
                    S.mm(lst, kkeys + ['qT', 'biasT'], [('ps', 2 * pp), ('ps', 2 * pp + 1)])
                    pt = PT[:, pp * 640:(pp + 1) * 640]
                    S.act(pt[:, s0 * 128:640], sc[:, s0 * 128:640], AF.Exp, [('ps', 2 * pp), ('ps', 2 * pp + 1)], [('PT', pp)])

                def pv(h):
                    pp = h % 2
                    pt = PT[:, pp * 640:(pp + 1) * 640]
                    o = bank(OB)[:, (h % 4) * 65:(h % 4 + 1) * 65]
                    lst = []
                    vkeys = []
                    for s in range(s0, 5):
                        ks = (i - 4 + s) % 8
                        vkeys.append(('v1', ks))
                        lst.append((o, pt[:, s * 128:(s + 1) * 128], v1v[:, ks, h, :], s == s0, s == 4))
                    S.mm(lst, vkeys + [('PT', pp)], [('ps', OB)])
                    if h % 4 == 3:
                        o3 = bank(OB)[:, 0:260].rearrange("p (h e) -> p h e", h=4)
                        r4 = rec[:, (h // 4) * 4:(h // 4) * 4 + 4]
                        S.op('dve', lambda e, r4=r4, o3=o3: e.reciprocal(out=r4, in_=o3[:, :, 64]), [('ps', OB)], ['rec'])
                        S.tt('dve', ym[:, 512 + (h // 4) * 256:512 + (h // 4 + 1) * 256].rearrange("p (h e) -> p h e", h=4), o3[:, :, 0:64],
                             r4.unsqueeze(2).to_broadcast([128, 4, 64]), ALU.mult, [('ps', OB), 'rec'], [('ymix', par)])

                scores(0)
                for h in range(1, 8):
                    scores(h)
                    if h % 2 == 1:
                        yield
                    pv(h - 1)
                yield
                pv(7)
                if 'ssd' not in flags:
                    S.memset('dve', ym[:, 0:512], 0.0, [('ymix', par)])
                if 'att' not in flags:
                    S.memset('dve', ym[:, 512:1024], 0.0, [('ymix', par)])
                bt = alloc1()
                pt16 = bank(bt).bitcast(BF16)
                S.tr([(pt16[:, c * 128:(c + 1) * 128], ym[:, c * 128:(c + 1) * 128], identb) for c in range(8)], [('ymix', par)], [('ps', bt)])
                S.cp('act', yT3[:, :, bc0:bc0 + 128], pt16.rearrange("p (c n) -> p c n", c=8), [('ps', bt)], ['yT'])
                free1(bt)
                yield

            def drive(gens):
                gens = list(gens)
                while gens:
                    for g_ in list(gens):
                        try:
                            next(g_)
                        except StopIteration:
                            gens.remove(g_)

            if not hoisted.pop('mix', False):
                norm_tile(0, lambda kc: Am4[:, l, kc, b:b + 1], lambda kc: mod4[:, l, kc, b:b + 1])
            for t in range(NTILE):
                c0 = t * TT
                projections(t)
                pump(8 if b == 0 else 0)
                for step in range(5):
                    gens = []
                    if step < 4:
                        gens.append(ssd_gen(t, step))
                    if step >= 1:
                        gens.append(att_gen(t, step - 1))
                    drive(gens)
                if t == 0:
                    dbgdump('yT', yT, ['yT'])
                for half in range(4):
                    slot, w3 = load_ring(wout_v, half * 256, wout_keys)
                    for oc in range(2):
                        m = half * 2 + oc
                        bo = alloc1()
                        S.mm([(bank(bo), w3[:, kc, oc * 128:(oc + 1) * 128], yT3[:, kc, :], kc == 0, kc == 7) for kc in range(8)],
                             ['yT', ('wring', slot)], [('ps', bo)])
                        S.stt('dve', xT3[:, m, c0:c0 + TT], bank(bo), mod4[:, l, 16 + m, b:b + 1], xT3[:, m, c0:c0 + TT], ALU.mult, ALU.add,
                              [('ps', bo), ('x', m, t)], [('x', m, t)])
                        free1(bo)
                    if half == 0:
                        if t + 1 < NTILE:
                            norm_tile(t + 1, lambda kc: Am4[:, l, kc, b:b + 1], lambda kc: mod4[:, l, kc, b:b + 1])
                        elif 'ffn' in flags:
                            norm_tile(0, lambda kc: Af4[:, l, kc, b:b + 1], lambda kc: mod4[:, l, 24 + kc, b:b + 1])
                            hoisted['ffn'] = True
            if 'xmix' in flags:
                dbgdump('xmix', xT, [('x', kc_, t_) for kc_ in range(8) for t_ in range(NTILE)])
            S.barrier()

        def ffn_phase(b, l):
            AR.off = phase_base
            singles[0] = list(range(8))
            aT = AR.alloc(NJ * TT, BF16)
            aT3 = aT.rearrange("p (j n) -> p j n", j=NJ)
            wupr = [AR.alloc(8 * 2 * 256, BF16) for _ in range(3)]
            wdnr = [AR.alloc(NJ * 256, BF16), AR.alloc(NJ * 256, BF16)]
            ug = AR.alloc(2 * 516, F32)
            uv = AR.alloc(2 * 516, F32)
            tg = AR.alloc(2 * TT, F32)
            tv = AR.alloc(2 * TT, F32)
            accg = AR.alloc(2 * TT, F32)
            accv = AR.alloc(2 * TT, F32)
            sg = AR.alloc(2 * TT, F32)
            uhalo = AR.alloc(44 * 2, F32)
            uhalo3 = uhalo.rearrange("p (c k) -> p c k", c=44)
            fc4 = fconvw.rearrange("p (l c k) -> p l c k", l=NL, c=44)
            S.memset('pool', uhalo, 0.0, ['uhalo'])
            wup_v = wup_s[l].rearrange("(kc p) n -> p kc n", p=128)
            wdn_v = wdn_s[l].rearrange("(jc p) n -> p jc n", p=128)
            wup_keys = [('wup', l, kc) for kc in range(8)]
            wdn_keys = [('wdn', l, jc) for jc in range(NJ)]
            while not all(k_ in converted for k_ in wup_keys + wdn_keys):
                pump(1)
            do_ada = (b == 0 and l + 1 < nlayers)
            if do_ada:
                ada_slots = [AR.alloc(8 * 256, BF16), AR.alloc(8 * 256, BF16)]
            ada_ch = [0]
            if not hoisted.pop('ffn', False):
                norm_tile(0, lambda kc: Af4[:, l, kc, b:b + 1], lambda kc: mod4[:, l, 24 + kc, b:b + 1])
            for t in range(NTILE):
                c0 = t * TT
                for jp in range(NJ // 2):
                    slot = (t * (NJ // 2) + jp) % 3
                    wu = wupr[slot].rearrange("p (k a n) -> p k a n", k=8, a=2)
                    for a in range(2):
                        S.dma('sp', wu[:, :, a, :], wup_v[:, :, a * FFN + jp * 256:a * FFN + (jp + 1) * 256], reads=wup_keys, writes=[('wupr', slot)])
                    for jj in range(2):
                        jc = 2 * jp + jj
                        par = jc % 2
                        ps_ = slice(par * TT, (par + 1) * TT)
                        raw = []
                        for a, ubuf, nm, tbuf, tnm in ((0, ug, 'ug', tg, 'tg'), (1, uv, 'uv', tv, 'tv')):
                            ch = a * NJ + jc
                            bu = nb()
                            S.mm([(bank(bu), wu[:, kc, a, jj * 128:(jj + 1) * 128], hT3[:, kc, :], kc == 0, kc == 7) for kc in range(8)],
                                 ['hT', ('wupr', slot)], [('ps', bu)])
                            ur = ubuf[:, par * 516:par * 516 + 514]
                            S.cp('pool', ur[:, 0:2], uhalo3[:, ch, :], ['uhalo'], [(nm, par)])
                            S.act(ur[:, 2:514], bank(bu), AF.Copy, [('ps', bu)], [(nm, par)])
                            S.act(tbuf[:, ps_], bank(bu), AF.Copy, [('ps', bu)], [(tnm, par)], scale=fc4[:, l, ch, 2:3])
                            S.cp('pool', uhalo3[:, ch, :], ur[:, 512:514], [(nm, par)], ['uhalo'])
                            raw.append((ur, ch, nm))
                        (urg, chg, _), (urv, chv, _) = raw
                        S.stt('dve', accg[:, ps_], urg[:, 0:512], fc4[:, l, chg, 0:1], tg[:, ps_], ALU.mult, ALU.add, [('ug', par), ('tg', par)], [('accg', par)])
                        S.stt('dve', accg[:, ps_], urg[:, 1:513], fc4[:, l, chg, 1:2], accg[:, ps_], ALU.mult, ALU.add, [('ug', par), ('accg', par)], [('accg', par)])
                        S.act(sg[:, ps_], accg[:, ps_], AF.Silu, [('accg', par)], [('sg', par)])
                        S.stt('dve', accv[:, ps_], urv[:, 0:512], fc4[:, l, chv, 0:1], tv[:, ps_], ALU.mult, ALU.add, [('uv', par), ('tv', par)], [('accv', par)])
                        S.stt('dve', accv[:, ps_], urv[:, 1:513], fc4[:, l, chv, 1:2], accv[:, ps_], ALU.mult, ALU.add, [('uv', par), ('accv', par)], [('accv', par)])
                        S.tt('dve', aT3[:, jc, :], sg[:, ps_], accv[:, ps_], ALU.mult, [('sg', par), ('accv', par)], [('aT', jc)])
                    if do_ada and jp % 2 == 1:
                        ada_chunk(l + 1, ada_ch[0], ada_slots)
                        ada_ch[0] += 1
                if do_ada:
                    ada_chunk(l + 1, ada_ch[0], ada_slots)
                    ada_ch[0] += 1
                pump(8 if b == 0 else 0)
                JS = 16
                for mp in range(4):
                    slot = mp % 2
                    wd = wdnr[slot].rearrange("p (j n) -> p j n", j=NJ)
                    S.dma('sp', wd, wdn_v[:, :, mp * 256:(mp + 1) * 256], reads=wdn_keys, writes=[('wdnr', slot)])
                    bos = [nb(), nb()]
                    if mp == 0:
                        for mm_ in range(2):
                            S.mm([(bank(bos[mm_]), wd[:, jc, mm_ * 128:(mm_ + 1) * 128], aT3[:, jc, :], jc == 0, False) for jc in range(JS)],
                                 [('aT', jc) for jc in range(JS)] + [('wdnr', slot)], [('ps', bos[mm_])])
                        if t + 1 < NTILE:
                            norm_tile(t + 1, lambda kc: Af4[:, l, kc, b:b + 1], lambda kc: mod4[:, l, 24 + kc, b:b + 1])
                        elif l + 1 < nlayers and ('ssd' in flags or 'att' in flags):
                            if do_ada:
                                assert ada_ch[0] == 24, ada_ch[0]
                                ada_finish(l + 1)
                                ada_ch[0] = -1
                            norm_tile(0, lambda kc: Am4[:, l + 1, kc, b:b + 1], lambda kc: mod4[:, l + 1, kc, b:b + 1],
                                      extra=[('Am', l + 1), ('mod', l + 1)])
                            hoisted['mix'] = True
                    for mm_ in range(2):
                        m = 2 * mp + mm_
                        bo = bos[mm_]
                        j0 = JS if mp == 0 else 0
                        S.mm([(bank(bo), wd[:, jc, mm_ * 128:(mm_ + 1) * 128], aT3[:, jc, :], jc == 0, jc == NJ - 1) for jc in range(j0, NJ)],
                             [('aT', jc) for jc in range(NJ)] + [('wdnr', slot)], [('ps', bo)])
                        S.stt('dve', xT3[:, m, c0:c0 + TT], bank(bo), mod4[:, l, 40 + m, b:b + 1], xT3[:, m, c0:c0 + TT], ALU.mult, ALU.add,
                              [('ps', bo), ('x', m, t)], [('x', m, t)])
            if do_ada and ada_ch[0] != -1:
                assert ada_ch[0] == 24, ada_ch[0]
                ada_finish(l + 1)
            S.barrier()

        for b in range(nseq):
            if b > 0:
                for kc in range(8):
                    S.dma('sp', xT3[:, kc, :], x_t[b, kc * 128:(kc + 1) * 128, :], writes=[('x', kc, t) for t in range(NTILE)])
            for l in range(nlayers):
                if 'ssd' in flags or 'att' in flags:
                    mixer_phase(b, l)
                if 'ffn' in flags:
                    ffn_phase(b, l)
            AR.off = phase_base
            ost = [AR.alloc(TT, F32), AR.alloc(TT, F32)]
            oi = 0
            for t in range(NTILE):
                c0 = t * TT
                bb = nb()
                for kc in range(8):
                    sl = sq[:, (kc % 2) * TT:(kc % 2 + 1) * TT]
                    S.act(sl, xT3[:, kc, c0:c0 + TT], AF.Square, [('x', kc, t)], [('sq', kc % 2)])
                    S.mm([(bank(bb), onesb, sl, kc == 0, kc == 7)], [('sq', kc % 2)], [('ps', bb)])
                S.act(rstd, bank(bb), AF.Ln, [('ps', bb)], ['rstd'], bias=EPS, scale=1.0 / D)
                S.act(rstd, rstd, AF.Exp, ['rstd'], ['rstd'], scale=-0.5)
                for kc in range(8):
                    o = ost[oi % 2]
                    S.stt('dve', o, xT3[:, kc, c0:c0 + TT], fnw[:, kc:kc + 1], rstd, ALU.mult, ALU.mult, [('x', kc, t), 'rstd'], [('ost', oi % 2)])
                    S.dma('sp', out_t[b, kc * 128:(kc + 1) * 128, c0:c0 + TT], o, reads=[('ost', oi % 2)], writes=[('out', b, kc, t)])
                    oi += 1
            S.barrier()
        S.finish([])
        print("sched: ops=%d waits=%d" % (S.nops, S.nwaits))
    return nc


def _shared_inputs(inp):
    L = NL
    f = np.float32
    k = np.arange(128)[:, None, None]
    s = np.arange(5)[None, :, None]
    q = np.arange(128)[None, None, :]
    dist = (4 - s) * 128 + q - k
    idx = np.clip(dist, -128, 128) + 128
    dc = 8 - 2 * s + q // 64 - k // 64
    valid = (dc >= 0) & (dc <= 8)
    g = np.asarray(inp["rel_bias"], f)[:, :, idx]
    g = np.where(valid[None, None], g, f(NEG))
    biasT = np.ascontiguousarray(g.transpose(0, 2, 1, 3, 4)).reshape(L, 128, 8 * 640)
    ones = np.ones((128, 128), f)
    sh = {
        "w_ada": np.ascontiguousarray(inp["w_ada"], f),
        "b_ada_t": np.ascontiguousarray(np.asarray(inp["b_ada"], f).reshape(L, 48, 128).transpose(2, 0, 1)).reshape(128, L * 48),
        "w_in": np.ascontiguousarray(inp["w_in"], f),
        "w_out": np.ascontiguousarray(inp["w_out"], f),
        "w_up": np.ascontiguousarray(inp["w_up"], f),
        "w_down": np.ascontiguousarray(inp["w_down"], f),
        "nmw_t": np.ascontiguousarray(np.asarray(inp["norm_mix_w"], f).reshape(L, 8, 128).transpose(2, 0, 1)).reshape(128, L * 8),
        "nfw_t": np.ascontiguousarray(np.asarray(inp["norm_ffn_w"], f).reshape(L, 8, 128).transpose(2, 0, 1)).reshape(128, L * 8),
        "fnw_t": np.ascontiguousarray(np.asarray(inp["final_norm_w"], f).reshape(8, 128).T),
        "convw_t": np.ascontiguousarray(np.asarray(inp["ssd_conv_w"], f).reshape(L, 4, 8, 128).transpose(3, 0, 2, 1)).reshape(128, L * 32),
        "convb_t": np.ascontiguousarray(np.asarray(inp["ssd_conv_b"], f).reshape(L, 8, 128).transpose(2, 0, 1)).reshape(128, L * 8),
        "fconvw_t": np.ascontiguousarray(np.asarray(inp["ffn_conv_w"], f).reshape(L, 3, 44, 128).transpose(3, 0, 2, 1)).reshape(128, L * 132),
        "dtb_bc": np.ascontiguousarray(np.broadcast_to(np.asarray(inp["dt_bias"], f).reshape(1, L * 8), (128, L * 8))),
        "alog_bc": np.ascontiguousarray(np.broadcast_to(np.asarray(inp["a_log"], f).reshape(1, L * 8), (128, L * 8))),
        "dskx": np.ascontiguousarray(np.broadcast_to(np.repeat(np.asarray(inp["d_skip"], f), 64, axis=1)[:, None, :], (L, 128, 512))),
        "ssdnw": np.ascontiguousarray(np.broadcast_to(np.asarray(inp["ssd_norm_w"], f)[:, None, :], (L, 128, 512))),
        "biasT": biasT,
        "identf": np.eye(128, dtype=f),
        "u1": np.ascontiguousarray(np.tril(ones, -1)),
        "u2": np.ascontiguousarray(np.triu(ones)),
        "mask01": np.ascontiguousarray(np.triu(ones)),
    }
    return sh


def _core_inputs(inp, sh, b0, nseq):
    f = np.float32
    m = dict(sh)
    m["x_t"] = np.ascontiguousarray(np.asarray(inp["x"][b0:b0 + nseq], f).transpose(0, 2, 1))
    m["c_t"] = np.ascontiguousarray(np.asarray(inp["c"][b0:b0 + nseq], f).reshape(nseq, 8, 128).transpose(2, 1, 0)).reshape(128, 8 * nseq)
    return m


_NC_CACHE = {}


def kernel(**inputs):
    n = 8
    nseq = 2
    sh = _shared_inputs(inputs)
    in_maps = [_core_inputs(inputs, sh, c * nseq, nseq) for c in range(n)]
    if "nc" not in _NC_CACHE:
        _NC_CACHE["nc"] = build_program(NL, nseq)
    res = run_bass_kernel_spmd(_NC_CACHE["nc"], in_maps, core_ids=list(range(n)))
    outs = [np.asarray(r["out_t"]).transpose(0, 2, 1) for r in res.results]
    return np.ascontiguousarray(np.concatenate(outs, axis=0).astype(np.float32))
```
